# Optimizing a Trainium2 kernel written in Bass

```python
import jax, jax.numpy as jnp
from jax import lax
import numpy as np

D_MODEL = 1024
BATCH = 8
SEQ = 8192
DEPTH = 1

CHUNK = 64
PLE_DIM = 256
MIX_WIDTH = D_MODEL
LRU_WIDTH = MIX_WIDTH // 2
LRU_BLOCKS = 8
LRU_BLOCK_W = LRU_WIDTH // LRU_BLOCKS
LRU_C = 8.0
CONV_W = 4
ATTN_WIDTH = MIX_WIDTH - LRU_WIDTH
HEAD_DIM = 64
N_HEADS = ATTN_WIDTH // HEAD_DIM
LEFT_CHUNKS = 8
BAND = (LEFT_CHUNKS + 1) * CHUNK
MAX_REL = 256
D_FF = 4 * D_MODEL
EPS = 1e-6
NEG_INF = -1e30
IN_WIDTH = 2 * LRU_WIDTH + 3 * ATTN_WIDTH

kernel_name = "hymba_rglru_chunk_attn_block"


def rms_norm(x, g):
    xf = x.astype(jnp.float32)
    y = xf * lax.rsqrt(jnp.mean(xf * xf, axis=-1, keepdims=True) + EPS)
    return (y * g.astype(jnp.float32)).astype(x.dtype)


def causal_depthwise_conv(x, w, b):
    s_len = x.shape[1]
    xp = jnp.pad(x, ((0, 0), (CONV_W - 1, 0), (0, 0)))
    y = xp[:, 0:s_len] * w[0]
    for tap in range(1, CONV_W):
        y = y + xp[:, tap:tap + s_len] * w[tap]
    return y + b


def block_diag_linear(x, w, b):
    bsz, s_len, _ = x.shape
    xb = x.reshape(bsz, s_len, LRU_BLOCKS, LRU_BLOCK_W)
    y = jnp.einsum("bsnc,ncd->bsnd", xb, w)
    return y.reshape(bsz, s_len, LRU_WIDTH) + b


def rg_lru(x, w_r, b_r, w_i, b_i, lam):
    xf = x.astype(jnp.float32)
    r = jax.nn.sigmoid(block_diag_linear(xf, w_r.astype(jnp.float32), b_r.astype(jnp.float32)))
    i = jax.nn.sigmoid(block_diag_linear(xf, w_i.astype(jnp.float32), b_i.astype(jnp.float32)))
    log_a = -LRU_C * r * jax.nn.softplus(-lam.astype(jnp.float32))
    a = jnp.exp(log_a)
    mult = jnp.sqrt(jnp.maximum(-jnp.expm1(2.0 * log_a), 0.0))
    u = mult * (i * xf)

    def combine(left, right):
        a1, b1 = left
        a2, b2 = right
        return a1 * a2, a2 * b1 + b2

    _, h = lax.associative_scan(combine, (a, u), axis=1)
    return h.astype(x.dtype)


def chunk_band_attention(q, k, v, bias, key_valid):
    s_len = q.shape[0]
    n_chunks = s_len // CHUNK
    qc = q.reshape(n_chunks, CHUNK, N_HEADS, HEAD_DIM)
    pad = ((LEFT_CHUNKS, 0), (0, 0), (0, 0), (0, 0))
    kp = jnp.pad(k.reshape(n_chunks, CHUNK, N_HEADS, HEAD_DIM), pad)
    vp = jnp.pad(v.reshape(n_chunks, CHUNK, N_HEADS, HEAD_DIM), pad)
    band_idx = jnp.arange(n_chunks)[:, None] + jnp.arange(LEFT_CHUNKS + 1)[None, :]
    kb = kp[band_idx].reshape(n_chunks, BAND, N_HEADS, HEAD_DIM)
    vb = vp[band_idx].reshape(n_chunks, BAND, N_HEADS, HEAD_DIM)
    scores = jnp.einsum("nqhd,nkhd->nhqk", qc, kb).astype(jnp.float32) * (HEAD_DIM ** -0.5)
    scores = scores + bias[None]
    scores = jnp.where(key_valid[:, None, None, :], scores, NEG_INF)
    probs = jax.nn.softmax(scores, axis=-1).astype(v.dtype)
    out = jnp.einsum("nhqk,nkhd->nqhd", probs, vb)
    return out.reshape(s_len, N_HEADS, HEAD_DIM)


def setup_inputs(seed: int = 0) -> dict:
    key = jax.random.key(seed)
    ks = jax.random.split(key, 24)
    f32 = jnp.float32

    def nrm(k, shape, scale):
        return jax.random.normal(k, shape, f32) * scale

    def gain(k, shape):
        return 1.0 + 0.02 * jax.random.normal(k, shape, f32)

    a0 = jax.random.uniform(ks[9], (DEPTH, LRU_WIDTH), f32, minval=0.9, maxval=0.999)
    s0 = a0 ** (1.0 / LRU_C)
    lru_lambda = jnp.log(s0) - jnp.log1p(-s0)
    return {
        "x": jax.random.normal(ks[0], (BATCH, SEQ, D_MODEL), f32),
        "p": jax.random.normal(ks[1], (DEPTH, BATCH, SEQ, PLE_DIM), f32),
        "norm_mix_g": gain(ks[2], (DEPTH, D_MODEL)),
        "w_in": nrm(ks[3], (DEPTH, D_MODEL, IN_WIDTH), D_MODEL ** -0.5),
        "conv_w": nrm(ks[4], (DEPTH, CONV_W, LRU_WIDTH), CONV_W ** -0.5),
        "conv_b": nrm(ks[5], (DEPTH, LRU_WIDTH), 0.02),
        "w_rg": nrm(ks[6], (DEPTH, LRU_BLOCKS, LRU_BLOCK_W, LRU_BLOCK_W), LRU_BLOCK_W ** -0.5),
        "b_rg": nrm(ks[7], (DEPTH, LRU_WIDTH), 0.02),
        "w_ig": nrm(ks[8], (DEPTH, LRU_BLOCKS, LRU_BLOCK_W, LRU_BLOCK_W), LRU_BLOCK_W ** -0.5),
        "b_ig": nrm(ks[10], (DEPTH, LRU_WIDTH), 0.02),
        "lru_lambda": lru_lambda,
        "q_norm_g": gain(ks[11], (DEPTH, HEAD_DIM)),
        "k_norm_g": gain(ks[12], (DEPTH, HEAD_DIM)),
        "rel_bias": nrm(ks[13], (DEPTH, N_HEADS, 2 * MAX_REL + 1), 0.1),
        "out_norm_lru_g": gain(ks[14], (DEPTH, LRU_WIDTH)),
        "out_norm_attn_g": gain(ks[15], (DEPTH, ATTN_WIDTH)),
        "w_out": nrm(ks[16], (DEPTH, MIX_WIDTH, D_MODEL), MIX_WIDTH ** -0.5),
        "norm_mlp_g": gain(ks[17], (DEPTH, D_MODEL)),
        "w_up": nrm(ks[18], (DEPTH, D_MODEL, D_FF), D_MODEL ** -0.5),
        "w_down": nrm(ks[19], (DEPTH, D_FF, D_MODEL), D_FF ** -0.5),
        "norm_ple_g": gain(ks[20], (DEPTH, D_MODEL)),
        "w_ple_gate": nrm(ks[21], (DEPTH, D_MODEL, D_MODEL), D_MODEL ** -0.5),
        "w_ple_proj": nrm(ks[22], (DEPTH, PLE_DIM, D_MODEL), PLE_DIM ** -0.5),
    }


def reference(x, p, norm_mix_g, w_in, conv_w, conv_b, w_rg, b_rg, w_ig, b_ig,
              lru_lambda, q_norm_g, k_norm_g, rel_bias, out_norm_lru_g,
              out_norm_attn_g, w_out, norm_mlp_g, w_up, w_down, norm_ple_g,
              w_ple_gate, w_ple_proj):
    bsz, s_len, _ = x.shape
    n_chunks = s_len // CHUNK

    q_in_chunk = jnp.arange(CHUNK)[:, None]
    k_in_band = jnp.arange(BAND)[None, :]
    rel = q_in_chunk + LEFT_CHUNKS * CHUNK - k_in_band
    rel_idx = jnp.clip(rel, -MAX_REL, MAX_REL) + MAX_REL
    key_chunk = jnp.arange(n_chunks)[:, None] - LEFT_CHUNKS + (jnp.arange(BAND) // CHUNK)[None, :]
    key_valid = key_chunk >= 0

    splits = [LRU_WIDTH, 2 * LRU_WIDTH, 2 * LRU_WIDTH + ATTN_WIDTH,
              2 * LRU_WIDTH + 2 * ATTN_WIDTH]

    h = x
    for i in range(DEPTH):
        u = rms_norm(h, norm_mix_g[i])
        proj = u @ w_in[i]
        x_lru, g_lru, q, k, v = jnp.split(proj, splits, axis=-1)

        x_lru = causal_depthwise_conv(x_lru, conv_w[i], conv_b[i])
        y_lru = rg_lru(x_lru, w_rg[i], b_rg[i], w_ig[i], b_ig[i], lru_lambda[i])
        y_lru = y_lru * jax.nn.gelu(g_lru)

        q = rms_norm(q.reshape(bsz, s_len, N_HEADS, HEAD_DIM), q_norm_g[i])
        k = rms_norm(k.reshape(bsz, s_len, N_HEADS, HEAD_DIM), k_norm_g[i])
        v = v.reshape(bsz, s_len, N_HEADS, HEAD_DIM)
        bias = rel_bias[i][:, rel_idx].astype(jnp.float32)
        y_attn = lax.map(lambda qkv: chunk_band_attention(qkv[0], qkv[1], qkv[2], bias, key_valid),
                         (q, k, v))
        y_attn = y_attn.reshape(bsz, s_len, ATTN_WIDTH)

        merged = jnp.concatenate([rms_norm(y_lru, out_norm_lru_g[i]),
                                  rms_norm(y_attn, out_norm_attn_g[i])], axis=-1)
        h = h + merged @ w_out[i]

        u = rms_norm(h, norm_mlp_g[i])
        h = h + jnp.square(jax.nn.relu(u @ w_up[i])) @ w_down[i]

        gate = jax.nn.sigmoid(rms_norm(h, norm_ple_g[i]) @ w_ple_gate[i])
        h = h + gate * (p[i] @ w_ple_proj[i])
    return h
```

```python
import numpy as np
import concourse.bass as bass
import concourse.mybir as mybir
from concourse.bass_utils import run_bass_kernel_spmd

F32 = mybir.dt.float32
BF16 = mybir.dt.bfloat16
AF = mybir.ActivationFunctionType
ALU = mybir.AluOpType

D = 1024
NPAR = 66
EPS = 1e-6
ENGS = ("pe", "act", "dve", "pool", "sp")


class Reg:
    __slots__ = ("name", "excl", "writer", "readers", "last")

    def __init__(self, name, excl=False):
        self.name = name
        self.excl = excl
        self.writer = None
        self.readers = []
        self.last = None


class Chan:
    def __init__(self, sem):
        self.sem = sem
        self.count = 0
        self.last = None


class Ins:
    __slots__ = ("eng", "fn", "deps", "sig", "signals", "chan", "dval", "is_dma")

    def __init__(self, eng, fn):
        self.eng = eng
        self.fn = fn
        self.deps = []
        self.sig = 0
        self.signals = False
        self.chan = None
        self.dval = 0
        self.is_dma = False


class Prog:
    def __init__(self, nc):
        self.nc = nc
        self.q = {e: [] for e in ENGS}
        self.sems = {}
        self.chans = []
        self.regs = {}

    def R(self, name, excl=False):
        r = self.regs.get(name)
        if r is None:
            r = Reg(name, excl)
            self.regs[name] = r
        return r

    def chan(self):
        c = Chan(None)
        self.chans.append(c)
        return c

    def add(self, eng, fn, reads=(), writes=(), chan=None):
        ins = Ins(eng, fn)
        deps = {}

        def dep(d, raw):
            if d is None or d is ins:
                return
            if d.is_dma:
                deps[id(d)] = d
                return
            if d.eng == eng and (eng == "pe" or not raw):
                return
            deps[id(d)] = d

        for r in reads:
            dep(r.writer, True)
            if r.excl:
                dep(r.last, False)
        for w in writes:
            dep(w.writer, True)
            for rd in w.readers:
                dep(rd, False)
            if w.excl:
                dep(w.last, False)
        for r in reads:
            r.readers.append(ins)
            r.last = ins
        for w in writes:
            w.writer = ins
            w.readers = []
            w.last = ins
        ins.deps = list(deps.values())
        for d in ins.deps:
            d.signals = True
        if chan is not None:
            ins.is_dma = True
            ins.chan = chan
            chan.count += 16
            ins.dval = chan.count
            chan.last = ins
        self.q[eng].append(ins)
        return ins

    def barrier(self, chans):
        lasts = []
        for e in ENGS:
            for ins in reversed(self.q[e]):
                if not ins.is_dma and ins.fn is not None:
                    lasts.append(ins)
                    break
        dl = [c.last for c in chans if c.last is not None]
        for e in ENGS:
            ins = Ins(e, None)
            ins.deps = [d for d in lasts if d.eng != e] + dl
            for d in ins.deps:
                d.signals = True
            self.q[e].append(ins)

    def emit(self, es, final_chans):
        nc = self.nc
        for e in ENGS:
            self.sems[e] = es.enter_context(nc.semaphore("s_" + e))
        for k, c in enumerate(self.chans):
            c.sem = es.enter_context(nc.semaphore("c%d" % k))
        for e in ENGS:
            n = 0
            for ins in self.q[e]:
                if ins.signals and not ins.is_dma:
                    n += 1
                    ins.sig = n
        engobj = {"pe": "tensor", "act": "scalar", "dve": "vector", "pool": "gpsimd", "sp": "sync"}
        with nc.Block() as block:
            for e in ENGS:
                def body(eng, e=e):
                    seen = {}
                    for ins in self.q[e]:
                        for d in ins.deps:
                            if d.is_dma:
                                sem, val = d.chan.sem, d.dval
                            else:
                                sem, val = self.sems[d.eng], d.sig
                            k = id(sem)
                            if seen.get(k, 0) >= val:
                                continue
                            seen[k] = val
                            eng.wait_ge(sem, val)
                        if ins.fn is None:
                            continue
                        bi = ins.fn(eng)
                        if ins.is_dma:
                            bi.then_inc(ins.chan.sem, 16)
                        elif ins.signals:
                            bi.then_inc(self.sems[e], 1)
                    if e == "sp":
                        for ch in final_chans:
                            eng.wait_ge(ch.sem, ch.count)
                getattr(block, engobj[e])(body)


def MM(out, lhsT, rhs, start, stop):
    return lambda e: e.matmul(out, lhsT, rhs, start=start, stop=stop)


def TR(out, in_, ident):
    return lambda e: e.transpose(out, in_, ident)


def ACT(out, in_, func, bias=None, scale=1.0, accum=None):
    def f(e):
        kw = {}
        if bias is not None:
            kw["bias"] = bias
        if accum is not None:
            kw["accum_out"] = accum
        return e.activation(out=out, in_=in_, func=func, scale=scale, **kw)
    return f


def TT(out, a, b, op):
    return lambda e: e.tensor_tensor(out, a, b, op)


def TS(out, a, s1, s2, op0, op1):
    return lambda e: e.tensor_scalar(out, a, s1, s2, op0, op1)


def TSM(out, a, s):
    return lambda e: e.tensor_scalar_mul(out, a, s)


def STT(out, a, s, b, op0, op1):
    return lambda e: e.scalar_tensor_tensor(out, a, s, b, op0, op1)


def CP(out, a):
    return lambda e: e.tensor_copy(out, a)


def MEMSET(ap, v):
    return lambda e: e.memset(ap, v)


def RECIP(out, a):
    return lambda e: e.reciprocal(out, a)


def SCAN(out, d0, d1, init, op0, op1):
    return lambda e: e.tensor_tensor_scan(out, d0, d1, init, op0, op1)


def DMA(out, in_):
    return lambda e: e.dma_start(out=out, in_=in_)


class Arena:
    def __init__(self, nc, base, limit):
        self.nc = nc
        self.off = base
        self.limit = limit
        self.n = 0

    def alloc(self, name, shape, dt):
        esz = 2 if dt == BF16 else 4
        size = esz
        for s in shape[1:]:
            size *= s
        size = (size + 63) // 64 * 64
        assert self.off + size <= self.limit, ("SBUF overflow", name, self.off, size, self.limit)
        self.n += 1
        t = self.nc.alloc_sbuf_tensor_at("%s_%d" % (name, self.n), list(shape), dt, offset=self.off)
        self.off += size
        return t


def build_program(S):
    from contextlib import ExitStack
    NT = S // 512
    NTB = S // 256
    nc = bass.Bass("TRN2", target_bir_lowering=False)

    def din(name, shape):
        return nc.dram_tensor(name, list(shape), F32, kind="ExternalInput").ap()

    x_d = din("x", [S, D])
    p_d = din("p", [S, 256])
    win_d = din("w_in", [D, 2560])
    wout_d = din("w_out", [D, D])
    wup_d = din("w_up", [D, 4096])
    wdn_d = din("w_down", [4096, D])
    wg_d = din("w_gate", [D, D])
    wp_d = din("w_ple", [256, D])
    wrg_d = din("wrg", [4, 128, 128])
    wig_d = din("wig", [4, 128, 128])
    par_d = din("par", [128, NPAR])
    btab_d = din("btab", [128, 8, 640])
    mask_d = din("mask", [128, 640])
    ident_d = din("ident", [128, 128])
    bones_d = din("bones", [128, 128])
    out_d = nc.dram_tensor("out", [S, D], F32, kind="ExternalOutput").ap()
    h1_d = nc.dram_tensor("h1s", [S, D], F32, kind="Internal").ap()

    P = Prog(nc)
    R = P.R
    base = (nc._sbuf_addr_for_side("left") + 63) // 64 * 64
    limit = nc._sbuf_addr_for_side("right")
    A0 = Arena(nc, base, limit)

    par = A0.alloc("par", [128, NPAR], F32)
    ident = A0.alloc("ident", [128, 128], BF16)
    bones = A0.alloc("bones", [128, 128], BF16)
    onesc = A0.alloc("onesc", [128, 2], BF16)
    sca = A0.alloc("sca", [128, 4], F32)
    gg = A0.alloc("gg", [128, 1], F32)
    carry = A0.alloc("carry", [128, 4], F32)
    ss1 = A0.alloc("ss1", [128, 4], F32)
    rs1 = A0.alloc("rs1", [128, 4], F32)
    ssA = A0.alloc("ssA", [128, 4], F32)
    rsA = A0.alloc("rsA", [128, 4], F32)
    rsL = A0.alloc("rsL", [128, 4], F32)
    rden = A0.alloc("rden", [128, 4], F32)
    junk = A0.alloc("junk", [128, 1024], BF16)
    XS = [A0.alloc("xs", [128, 1024], BF16) for _ in range(2)]
    common_end = A0.off

    banks = [nc.alloc_psum_tensor("bank%d" % b, [128, 512], F32) for b in range(8)]
    PB = [R("bank%d" % b, True) for b in range(8)]

    def bf(b):
        return banks[b][:]

    def bb(b):
        return banks[b][:].bitcast(BF16)

    rot_state = [0]

    def rot():
        b = rot_state[0]
        rot_state[0] = (b + 1) % 4
        return b

    A = Arena(nc, common_end, limit)
    Win = A.alloc("Win", [128, 8, 2560], BF16)
    Wout = A.alloc("Wout", [128, 8, 1024], BF16)
    Wrg = A.alloc("Wrg", [128, 4, 128], BF16)
    Wig = A.alloc("Wig", [128, 4, 128], BF16)
    XH = [A.alloc("xh", [128, 4, 1024], F32) for _ in range(2)]
    uT = A.alloc("uT", [128, 8, 512], BF16)
    xle = A.alloc("xle", [128, 4, 516], F32)
    xc = [A.alloc("xc", [128, 512], F32) for _ in range(2)]
    xcb = [A.alloc("xcb", [128, 512], BF16) for _ in range(2)]
    rr0 = A.alloc("rr", [128, 512], F32)
    rr = [rr0, rr0]
    ig = [A.alloc("ig", [128, 512], F32) for _ in range(2)]
    aa = [A.alloc("aa", [128, 512], F32) for _ in range(2)]
    t1 = [A.alloc("t1", [128, 512], F32) for _ in range(2)]
    hh = [A.alloc("hh", [128, 512], F32) for _ in range(2)]
    gl = [A.alloc("gl", [128, 512], F32) for _ in range(2)]
    sqy = A.alloc("sqy", [128, 4, 512], BF16)
    mst_off = A.off
    qraw = [A.alloc("qraw", [128, 512], F32) for _ in range(2)]
    sqq = [A.alloc("sqq", [128, 512], BF16) for _ in range(2)]
    rsq = [A.alloc("rsq", [128, 512], F32) for _ in range(2)]
    qT = A.alloc("qT", [128, 4, 512], BF16)
    kT = A.alloc("kT", [128, 4, 1024], BF16)
    Vr = A.alloc("Vr", [128, 8, 8, 65], BF16)
    E = A.alloc("E", [128, 8, 640], BF16)
    stage_off = A.off
    Pb = [A.alloc("Pb", [128, 2560], BF16) for _ in range(2)]
    ygT = A.alloc("ygT", [128, 8, 512], BF16)
    yat = A.alloc("yat", [128, 4, 512], BF16)
    bst = nc.alloc_sbuf_tensor_at("bstage", [128, 8, 640], F32, offset=stage_off)
    mst = nc.alloc_sbuf_tensor_at("mstage", [128, 640], F32, offset=mst_off)
    assert stage_off + 8 * 640 * 4 <= A.off

    B = Arena(nc, common_end, limit)
    Wup = B.alloc("Wup", [128, 8, 4096], BF16)
    Wdn = B.alloc("Wdn", [128, 32, 1024], BF16)
    Wg = B.alloc("Wg", [128, 8, 1024], BF16)
    Wp = B.alloc("Wp", [128, 2, 1024], BF16)
    HB = [B.alloc("hb", [128, 2, 1024], F32) for _ in range(2)]
    PBF = [B.alloc("pb", [128, 2, 256], F32) for _ in range(2)]
    u2T = B.alloc("u2T", [128, 8, 256], BF16)
    rl = [B.alloc("rl", [128, 256], BF16) for _ in range(4)]
    hid = [B.alloc("hid", [128, 256], BF16) for _ in range(4)]
    gt = [B.alloc("gt", [128, 512], F32) for _ in range(2)]
    tmp = [B.alloc("tmp", [128, 512], F32) for _ in range(2)]
    pbb = [B.alloc("pbb", [128, 256], BF16) for _ in range(2)]
    pT = B.alloc("pT", [128, 2, 256], BF16)
    ssB = B.alloc("ssB", [128, 2], F32)
    rsB = B.alloc("rsB", [128, 2], F32)

    c_par = P.chan()
    c_small = P.chan()
    c_stage = P.chan()
    c_stage2 = P.chan()
    P.add("sp", DMA(par[:], par_d[:]), writes=[R("par")], chan=c_par)
    P.add("sp", DMA(bst[:], btab_d[:]), writes=[R("bst")], chan=c_stage)
    P.add("sp", DMA(mst[:], mask_d[:]), writes=[R("mst")], chan=c_stage2)
    P.add("pool", DMA(ident[:], ident_d[:]), writes=[R("ident")], chan=c_small)
    P.add("pool", DMA(bones[:], bones_d[:]), writes=[R("bones")], chan=c_small)
    P.add("pool", DMA(Wrg[:], wrg_d.rearrange("c p n -> p c n")), writes=[R("wrg")], chan=c_small)
    P.add("pool", DMA(Wig[:], wig_d.rearrange("c p n -> p c n")), writes=[R("wig")], chan=c_small)
    c_x = [P.chan(), P.chan()]

    def xh_regs(buf):
        return [R("xh%d_%d" % (buf, tg)) for tg in range(4)]

    def load_x(i):
        buf = i % 2
        src = x_d[i * 512:(i + 1) * 512, :].rearrange("(g p) f -> p g f", p=128)
        P.add("sp", DMA(XH[buf][:], src), writes=xh_regs(buf), chan=c_x[buf])

    load_x(0)
    win_v = win_d.rearrange("(k p) n -> p k n", p=128)
    c_win = []
    for pc in range(5):
        c = P.chan()
        c_win.append(c)
        P.add("pool", DMA(Win[:, :, pc * 512:(pc + 1) * 512], win_v[:, :, pc * 512:(pc + 1) * 512]),
              writes=[R("win%d" % pc)], chan=c)
    c_wout = P.chan()
    P.add("pool", DMA(Wout[:], wout_d.rearrange("(k p) n -> p k n", p=128)), writes=[R("wout")], chan=c_wout)

    P.add("dve", MEMSET(carry[:], 0.0), writes=[R("carry")])
    P.add("dve", MEMSET(xle[:], 0.0), writes=[R("xle%d" % c) for c in range(4)])
    P.add("dve", MEMSET(onesc[:], 1.0), writes=[R("onesc")])
    P.add("dve", MEMSET(Vr[:], 1.0), writes=[R("V%d" % s) for s in range(8)])
    P.add("act", ACT(sca[:], par[:, 52:56], AF.Exp, scale=-1.0), reads=[R("par")], writes=[R("sca")])
    P.add("act", ACT(sca[:], sca[:], AF.Ln, bias=1.0), reads=[R("sca")], writes=[R("sca")])
    P.add("dve", TSM(sca[:], sca[:], -8.0), reads=[R("sca")], writes=[R("sca")])
    P.add("dve", STT(gg[:], par[:, 64:65], 0.125, par[:, 65:66], ALU.mult, ALU.mult),
          reads=[R("par")], writes=[R("gg")])
    for h in range(8):
        P.add("act", ACT(bst[:, h, :], bst[:, h, :], AF.Exp), reads=[R("bst")], writes=[R("bst")])
        P.add("dve", TT(E[:, h, :], bst[:, h, :], mst[:], ALU.mult), reads=[R("bst"), R("mst")],
              writes=[R("E")])
    P.barrier([c_par, c_small, c_stage, c_stage2])

    c_h1s = [P.chan(), P.chan()]
    widths = []
    for b in range(8):
        ql = max(0, 2 * b - 8)
        qh = min(7, 2 * b + 1)
        widths.append((ql, (qh - ql + 1) * 64))
    poff = [0]
    for b in range(8):
        poff.append(poff[-1] + widths[b][1])

    def inproj_ws(col0, bank):
        pc = col0 // 512
        for k in range(8):
            P.add("pe", MM(bf(bank), Win[:, k, col0:col0 + 128], uT[:, k, :], k == 0, k == 7),
                  reads=[R("uT%d" % k), R("win%d" % pc)], writes=[PB[bank]])

    def pass_a_tile(i):
        cur = i % 2
        prv = 1 - cur
        xh = XH[cur]
        xr = xh_regs(cur)
        if i + 1 < NT:
            load_x(i + 1)
        P.add("dve", MEMSET(ss1[:], 0.0), writes=[R("ss1")])
        for tg in range(4):
            P.add("act", ACT(junk[:], xh[:, tg, :], AF.Square, accum=ss1[:, tg:tg + 1]),
                  reads=[xr[tg]], writes=[R("junk"), R("ss1")])
        P.add("act", ACT(rs1[:], ss1[:], AF.Ln, bias=EPS, scale=1.0 / D), reads=[R("ss1")], writes=[R("rs1")])
        P.add("act", ACT(rs1[:], rs1[:], AF.Exp, scale=-0.5), reads=[R("rs1")], writes=[R("rs1")])
        for tg in range(4):
            xs = XS[tg % 2]
            xsr = R("xs%d" % (tg % 2))
            P.add("dve", TSM(xs[:], xh[:, tg, :], rs1[:, tg:tg + 1]), reads=[xr[tg], R("rs1")], writes=[xsr])
            for c in range(8):
                bank = 4 + c // 2
                o = (c % 2) * 512 + tg * 128
                P.add("pe", TR(bb(bank)[:, o:o + 128], xs[:, c * 128:(c + 1) * 128], ident[:]),
                      reads=[xsr, R("ident")], writes=[PB[bank]])
        for c in range(8):
            bank = 4 + c // 2
            o = (c % 2) * 512
            P.add("dve", TSM(uT[:, c, :], bb(bank)[:, o:o + 512], par[:, c:c + 1]),
                  reads=[PB[bank], R("par")], writes=[R("uT%d" % c)])

        def lru_front(c):
            s = c % 2
            b1 = rot()
            inproj_ws(c * 128, b1)
            P.add("act", ACT(xle[:, c, 3:515], bf(b1), AF.Copy), reads=[PB[b1]], writes=[R("xle%d" % c)])
            b2 = rot()
            inproj_ws(512 + c * 128, b2)
            P.add("act", ACT(gl[s][:], bf(b2), AF.Gelu_apprx_tanh), reads=[PB[b2]], writes=[R("gl%d" % s)])
            rx = [R("xle%d" % c), R("par")]
            P.add("dve", TS(xc[s][:], xle[:, c, 0:512], par[:, 24 + c * 4:25 + c * 4], par[:, 40 + c:41 + c],
                            ALU.mult, ALU.add), reads=rx, writes=[R("xc%d" % s)])
            for k in range(1, 4):
                P.add("dve", STT(xc[s][:], xle[:, c, k:k + 512], par[:, 24 + c * 4 + k:25 + c * 4 + k], xc[s][:],
                                 ALU.mult, ALU.add), reads=rx + [R("xc%d" % s)], writes=[R("xc%d" % s)])
            P.add("dve", CP(xle[:, c, 0:3], xle[:, c, 512:515]), reads=[R("xle%d" % c)], writes=[R("xle%d" % c)])
            P.add("act", ACT(xcb[s][:], xc[s][:], AF.Copy), reads=[R("xc%d" % s)], writes=[R("xcb%d" % s)])

        def lru_back(c):
            s = c % 2
            br = rot()
            P.add("pe", MM(bf(br), Wrg[:, c, :], xcb[s][:], True, True), reads=[R("xcb%d" % s), R("wrg")],
                  writes=[PB[br]])
            P.add("act", ACT(rr[s][:], bf(br), AF.Sigmoid, bias=par[:, 44 + c:45 + c]), reads=[PB[br], R("par")],
                  writes=[R("rr0")])
            bi = rot()
            P.add("pe", MM(bf(bi), Wig[:, c, :], xcb[s][:], True, True), reads=[R("xcb%d" % s), R("wig")],
                  writes=[PB[bi]])
            P.add("act", ACT(ig[s][:], bf(bi), AF.Sigmoid, bias=par[:, 48 + c:49 + c]), reads=[PB[bi], R("par")],
                  writes=[R("ig%d" % s)])
            P.add("act", ACT(aa[s][:], rr[s][:], AF.Exp, scale=sca[:, c:c + 1]), reads=[R("rr0"), R("sca")],
                  writes=[R("aa%d" % s)])
            P.add("dve", STT(t1[s][:], aa[s][:], -1.0, aa[s][:], ALU.mult, ALU.mult), reads=[R("aa%d" % s)],
                  writes=[R("t1%d" % s)])
            P.add("dve", TS(t1[s][:], t1[s][:], 1.0, 0.0, ALU.add, ALU.max), reads=[R("t1%d" % s)],
                  writes=[R("t1%d" % s)])
            P.add("act", ACT(t1[s][:], t1[s][:], AF.Sqrt), reads=[R("t1%d" % s)], writes=[R("t1%d" % s)])
            P.add("dve", TT(ig[s][:], ig[s][:], xc[s][:], ALU.mult), reads=[R("ig%d" % s), R("xc%d" % s)],
                  writes=[R("ig%d" % s)])
            P.add("dve", TT(ig[s][:], ig[s][:], t1[s][:], ALU.mult), reads=[R("ig%d" % s), R("t1%d" % s)],
                  writes=[R("ig%d" % s)])
            P.add("dve", SCAN(hh[s][:], aa[s][:], ig[s][:], carry[:, c:c + 1], ALU.mult, ALU.add),
                  reads=[R("aa%d" % s), R("ig%d" % s), R("carry")], writes=[R("hh%d" % s)])
            P.add("dve", CP(carry[:, c:c + 1], hh[s][:, 511:512]), reads=[R("hh%d" % s)], writes=[R("carry")])
            P.add("dve", TT(gl[s][:], gl[s][:], hh[s][:], ALU.mult), reads=[R("gl%d" % s), R("hh%d" % s)],
                  writes=[R("gl%d" % s)])
            P.add("act", ACT(sqy[:, c, :], gl[s][:], AF.Square), reads=[R("gl%d" % s)], writes=[R("sqy%d" % c)])
            P.add("dve", TSM(ygT[:, c, :], gl[s][:], par[:, 56 + c:57 + c]), reads=[R("gl%d" % s), R("par")],
                  writes=[R("ygT%d" % c)])

        for c in range(4):
            lru_front(c)
            if c >= 1:
                lru_back(c - 1)

        def qk_front(n):
            s = n % 2
            col0 = 1024 + n * 128
            b = rot()
            inproj_ws(col0, b)
            P.add("dve", CP(qraw[s][:], bf(b)), reads=[PB[b]], writes=[R("qraw%d" % s)])
            P.add("act", ACT(sqq[s][:], qraw[s][:], AF.Square), reads=[R("qraw%d" % s)], writes=[R("sqq%d" % s)])

        def qk_back(n):
            s = n % 2
            b2 = rot()
            P.add("pe", MM(bf(b2), bones[:], sqq[s][:], True, True), reads=[R("sqq%d" % s), R("bones")],
                  writes=[PB[b2]])
            P.add("act", ACT(rsq[s][:], bf(b2), AF.Ln, bias=EPS, scale=1.0 / 64), reads=[PB[b2]],
                  writes=[R("rsq%d" % s)])
            P.add("act", ACT(rsq[s][:], rsq[s][:], AF.Exp, scale=-0.5), reads=[R("rsq%d" % s)],
                  writes=[R("rsq%d" % s)])
            if n < 4:
                P.add("dve", STT(qT[:, n, :], qraw[s][:], gg[:, 0:1], rsq[s][:], ALU.mult, ALU.mult),
                      reads=[R("qraw%d" % s), R("gg"), R("rsq%d" % s)], writes=[R("qT%d" % n)])
            else:
                j = n - 4
                P.add("dve", TT(kT[:, j, cur * 512:(cur + 1) * 512], qraw[s][:], rsq[s][:], ALU.mult),
                      reads=[R("qraw%d" % s), R("rsq%d" % s)], writes=[R("kT%d_%d" % (j, cur))])

        qk_front(0)
        lru_back(3)
        for n in range(1, 8):
            qk_front(n)
            qk_back(n - 1)
        for tg in range(4):
            b = rot()
            for k in range(8):
                P.add("pe", MM(bf(b), uT[:, k, tg * 128:(tg + 1) * 128], Win[:, k, 2048:2560], k == 0, k == 7),
                      reads=[R("uT%d" % k), R("win4")], writes=[PB[b]])
            slot = 4 * cur + tg
            P.add("act", ACT(Vr[:, slot, :, 0:64], bf(b).rearrange("p (h d) -> p h d", h=8), AF.Copy),
                  reads=[PB[b]], writes=[R("V%d" % slot)])
            if tg == 0:
                qk_back(7)

        bs = rot()
        for tg in range(4):
            for c in range(4):
                P.add("pe", MM(bf(bs)[:, tg:tg + 1], sqy[:, c, tg * 128:(tg + 1) * 128], onesc[:, 0:1], c == 0, c == 3),
                      reads=[R("sqy%d" % c), R("onesc")], writes=[PB[bs]])
        P.add("act", ACT(rsL[:], bf(bs)[:, 0:4], AF.Ln, bias=EPS, scale=1.0 / 512), reads=[PB[bs]], writes=[R("rsL")])
        P.add("act", ACT(rsL[:], rsL[:], AF.Exp, scale=-0.5), reads=[R("rsL")], writes=[R("rsL")])

        blocks = [b for b in range(8) if (i > 0 or b >= 4)]
        for h in range(8):
            j = h // 2
            ro = 64 * (h % 2)
            st = h % 2
            hq = h % 4
            for b in blocks:
                ql, w = widths[b]
                half = prv if b < 4 else cur
                kcol = half * 512 + (b % 4) * 128
                bank = rot()
                P.add("pe", MM(bf(bank)[:, 0:w], kT[ro:ro + 64, j, kcol:kcol + 128],
                               qT[ro:ro + 64, j, ql * 64:ql * 64 + w], True, True),
                      reads=[R("kT%d_%d" % (j, half)), R("qT%d" % j)], writes=[PB[bank]])
                pr = R("P%d_%d" % (st, b))
                pv = Pb[st][:, poff[b]:poff[b] + w]
                P.add("act", ACT(pv, bf(bank)[:, 0:w], AF.Exp), reads=[PB[bank]], writes=[pr])
                ec = (ql + 8 - 2 * b) * 64
                P.add("dve", TT(pv, pv, E[:, h, ec:ec + w], ALU.mult), reads=[pr, R("E")], writes=[pr])
            for g in range(4):
                bl = [b for b in range(g, g + 5) if b in blocks]
                for n, b in enumerate(bl):
                    ql, w = widths[b]
                    half = prv if b < 4 else cur
                    slot = 4 * half + (b % 4)
                    pcn = poff[b] + (2 * g - ql) * 64
                    P.add("pe", MM(bf(4 + g)[:, hq * 65:(hq + 1) * 65], Pb[st][:, pcn:pcn + 128], Vr[:, slot, h, :],
                                   n == 0, n == len(bl) - 1),
                          reads=[R("P%d_%d" % (st, b)), R("V%d" % slot)], writes=[PB[4 + g]])
            if hq == 3:
                hb0 = (h // 4) * 4
                for g in range(4):
                    v3 = bf(4 + g)[:, 0:260].rearrange("p (h e) -> p h e", e=65)
                    P.add("dve", RECIP(rden[:], v3[:, :, 64]), reads=[PB[4 + g]], writes=[R("rden")])
                    P.add("dve", TT(yat[:, g, hb0 * 64:(hb0 + 4) * 64].rearrange("p (h d) -> p h d", d=64),
                                    v3[:, :, 0:64], rden[:].unsqueeze(2).to_broadcast([128, 4, 64]), ALU.mult),
                          reads=[PB[4 + g], R("rden")], writes=[R("yat%d" % g)])
        P.add("dve", MEMSET(ssA[:], 0.0), writes=[R("ssA")])
        for g in range(4):
            P.add("act", ACT(junk[:, 0:512], yat[:, g, :], AF.Square, accum=ssA[:, g:g + 1]),
                  reads=[R("yat%d" % g)], writes=[R("junk"), R("ssA")])
        P.add("act", ACT(rsA[:], ssA[:], AF.Ln, bias=EPS, scale=1.0 / 512), reads=[R("ssA")], writes=[R("rsA")])
        P.add("act", ACT(rsA[:], rsA[:], AF.Exp, scale=-0.5), reads=[R("rsA")], writes=[R("rsA")])
        for fc in range(4):
            bank = 4 + fc // 2
            for g in range(4):
                o = (fc % 2) * 512 + g * 128
                P.add("pe", TR(bb(bank)[:, o:o + 128], yat[:, g, fc * 128:(fc + 1) * 128], ident[:]),
                      reads=[R("yat%d" % g), R("ident")], writes=[PB[bank]])
        for fc in range(4):
            bank = 4 + fc // 2
            o = (fc % 2) * 512
            P.add("dve", TSM(ygT[:, 4 + fc, :], bb(bank)[:, o:o + 512], par[:, 60 + fc:61 + fc]),
                  reads=[PB[bank], R("par")], writes=[R("ygT%d" % (4 + fc))])
        for tg in range(4):
            for half in range(2):
                hv = xh[:, tg, half * 512:(half + 1) * 512]
                bA = rot()
                for k in range(4):
                    P.add("pe", MM(bf(bA), ygT[:, k, tg * 128:(tg + 1) * 128], Wout[:, k, half * 512:(half + 1) * 512],
                                   k == 0, k == 3), reads=[R("ygT%d" % k), R("wout")], writes=[PB[bA]])
                bB = rot()
                for k in range(4, 8):
                    P.add("pe", MM(bf(bB), ygT[:, k, tg * 128:(tg + 1) * 128], Wout[:, k, half * 512:(half + 1) * 512],
                                   k == 4, k == 7), reads=[R("ygT%d" % k), R("wout")], writes=[PB[bB]])
                P.add("dve", STT(hv, bf(bA), rsL[:, tg:tg + 1], hv, ALU.mult, ALU.add),
                      reads=[PB[bA], R("rsL"), xr[tg]], writes=[xr[tg]])
                P.add("dve", STT(hv, bf(bB), rsA[:, tg:tg + 1], hv, ALU.mult, ALU.add),
                      reads=[PB[bB], R("rsA"), xr[tg]], writes=[xr[tg]])
        dst = h1_d[i * 512:(i + 1) * 512, :].rearrange("(g p) f -> p g f", p=128)
        P.add("sp", DMA(dst, xh[:]), reads=xr, writes=[R("h1_%d" % i)], chan=c_h1s[cur])

    for i in range(NT):
        pass_a_tile(i)

    P.barrier(c_h1s + c_x + c_win + [c_wout])

    c_hb = [P.chan(), P.chan()]
    c_pb = [P.chan(), P.chan()]
    c_out = [P.chan(), P.chan()]
    wup_v = wup_d.rearrange("(k p) n -> p k n", p=128)
    wdn_v = wdn_d.rearrange("(f p) n -> p f n", p=128)
    c_w = []

    def hb_regs(buf):
        return [R("hb%d_%d" % (buf, tg)) for tg in range(2)]

    def load_b(n):
        buf = n % 2
        src = h1_d[n * 256:(n + 1) * 256, :].rearrange("(g p) f -> p g f", p=128)
        P.add("sp", DMA(HB[buf][:], src), reads=[R("h1_%d" % (n // 2))], writes=hb_regs(buf), chan=c_hb[buf])
        srcp = p_d[n * 256:(n + 1) * 256, :].rearrange("(g p) f -> p g f", p=128)
        P.add("sp", DMA(PBF[buf][:], srcp), writes=[R("pb%d" % buf)], chan=c_pb[buf])

    load_b(0)
    for pc in range(8):
        c = P.chan()
        c_w.append(c)
        P.add("pool", DMA(Wup[:, :, pc * 512:(pc + 1) * 512], wup_v[:, :, pc * 512:(pc + 1) * 512]),
              writes=[R("wup%d" % pc)], chan=c)
        c = P.chan()
        c_w.append(c)
        P.add("pool", DMA(Wdn[:, pc * 4:(pc + 1) * 4, :], wdn_v[:, pc * 4:(pc + 1) * 4, :]),
              writes=[R("wdn%d" % pc)], chan=c)
    c = P.chan()
    c_w.append(c)
    P.add("pool", DMA(Wg[:], wg_d.rearrange("(k p) n -> p k n", p=128)), writes=[R("wg")], chan=c)
    c = P.chan()
    c_w.append(c)
    P.add("pool", DMA(Wp[:], wp_d.rearrange("(k p) n -> p k n", p=128)), writes=[R("wp")], chan=c)

    def norm_b(hb, hr, gcol):
        P.add("dve", MEMSET(ssB[:], 0.0), writes=[R("ssB")])
        for tg in range(2):
            P.add("act", ACT(junk[:], hb[:, tg, :], AF.Square, accum=ssB[:, tg:tg + 1]),
                  reads=[hr[tg]], writes=[R("junk"), R("ssB")])
        P.add("act", ACT(rsB[:], ssB[:], AF.Ln, bias=EPS, scale=1.0 / D), reads=[R("ssB")], writes=[R("rsB")])
        P.add("act", ACT(rsB[:], rsB[:], AF.Exp, scale=-0.5), reads=[R("rsB")], writes=[R("rsB")])
        bT = [rot(), rot()]
        for tg in range(2):
            xs = XS[tg]
            xsr = R("xs%d" % tg)
            P.add("dve", TSM(xs[:], hb[:, tg, :], rsB[:, tg:tg + 1]), reads=[hr[tg], R("rsB")], writes=[xsr])
            for c in range(8):
                bank = bT[c // 4]
                o = (c % 4) * 256 + tg * 128
                P.add("pe", TR(bb(bank)[:, o:o + 128], xs[:, c * 128:(c + 1) * 128], ident[:]),
                      reads=[xsr, R("ident")], writes=[PB[bank]])
        for c in range(8):
            bank = bT[c // 4]
            o = (c % 4) * 256
            P.add("dve", TSM(u2T[:, c, :], bb(bank)[:, o:o + 256], par[:, gcol + c:gcol + c + 1]),
                  reads=[PB[bank], R("par")], writes=[R("u2T")])

    def pass_b_tile(n):
        cur = n % 2
        hb = HB[cur]
        hr = hb_regs(cur)
        pbf = PBF[cur]
        if n + 1 < NTB:
            load_b(n + 1)
        norm_b(hb, hr, 8)

        def up(f):
            b = rot()
            for k in range(8):
                P.add("pe", MM(bf(b)[:, 0:256], Wup[:, k, f * 128:(f + 1) * 128], u2T[:, k, :], k == 0, k == 7),
                      reads=[R("u2T"), R("wup%d" % (f // 4))], writes=[PB[b]])
            P.add("act", ACT(rl[f % 4][:], bf(b)[:, 0:256], AF.Relu), reads=[PB[b]], writes=[R("rl%d" % (f % 4))])
            P.add("dve", TT(hid[f % 4][:], rl[f % 4][:], rl[f % 4][:], ALU.mult), reads=[R("rl%d" % (f % 4))],
                  writes=[R("hid%d" % (f % 4))])

        def down(f):
            for tg in range(2):
                for half in range(2):
                    bk = 4 + tg * 2 + half
                    P.add("pe", MM(bf(bk), hid[f % 4][:, tg * 128:(tg + 1) * 128],
                                   Wdn[:, f, half * 512:(half + 1) * 512], f == 0, f == 31),
                          reads=[R("hid%d" % (f % 4)), R("wdn%d" % (f // 4))], writes=[PB[bk]])

        up(0)
        up(1)
        for f in range(32):
            if f + 2 < 32:
                up(f + 2)
            down(f)
        for tg in range(2):
            for half in range(2):
                bk = 4 + tg * 2 + half
                hv = hb[:, tg, half * 512:(half + 1) * 512]
                P.add("dve", TT(hv, bf(bk), hv, ALU.add), reads=[hr[tg], PB[bk]], writes=[hr[tg]])
        norm_b(hb, hr, 16)
        bt = rot()
        for tg in range(2):
            P.add("dve", CP(pbb[tg][:], pbf[:, tg, :]), reads=[R("pb%d" % cur)], writes=[R("pbb%d" % tg)])
            for kc in range(2):
                o = kc * 256 + tg * 128
                P.add("pe", TR(bb(bt)[:, o:o + 128], pbb[tg][:, kc * 128:(kc + 1) * 128], ident[:]),
                      reads=[R("pbb%d" % tg), R("ident")], writes=[PB[bt]])
        P.add("act", ACT(pT[:].rearrange("p k t -> p (k t)"), bb(bt)[:, 0:512], AF.Copy), reads=[PB[bt]],
              writes=[R("pT")])
        idx = 0
        for tg in range(2):
            for half in range(2):
                s = idx % 2
                idx += 1
                hv = hb[:, tg, half * 512:(half + 1) * 512]
                bgt = rot()
                for k in range(8):
                    P.add("pe", MM(bf(bgt), u2T[:, k, tg * 128:(tg + 1) * 128], Wg[:, k, half * 512:(half + 1) * 512],
                                   k == 0, k == 7), reads=[R("u2T"), R("wg")], writes=[PB[bgt]])
                P.add("act", ACT(gt[s][:], bf(bgt), AF.Sigmoid), reads=[PB[bgt]], writes=[R("gt%d" % s)])
                bpp = rot()
                for kc in range(2):
                    P.add("pe", MM(bf(bpp), pT[:, kc, tg * 128:(tg + 1) * 128], Wp[:, kc, half * 512:(half + 1) * 512],
                                   kc == 0, kc == 1), reads=[R("pT"), R("wp")], writes=[PB[bpp]])
                P.add("dve", TT(tmp[s][:], bf(bpp), gt[s][:], ALU.mult), reads=[R("gt%d" % s), PB[bpp]],
                      writes=[R("tmp%d" % s)])
                P.add("dve", TT(hv, hv, tmp[s][:], ALU.add), reads=[hr[tg], R("tmp%d" % s)], writes=[hr[tg]])
        dst = out_d[n * 256:(n + 1) * 256, :].rearrange("(g p) f -> p g f", p=128)
        P.add("sp", DMA(dst, hb[:]), reads=hr, chan=c_out[cur])

    for n in range(NTB):
        pass_b_tile(n)

    es = ExitStack()
    with es:
        P.emit(es, c_out)
    return nc


_CACHE = {}


def _get_program(S):
    if S not in _CACHE:
        _CACHE[S] = build_program(S)
    return _CACHE[S]


def kernel(x, p, norm_mix_g, w_in, conv_w, conv_b, w_rg, b_rg, w_ig, b_ig, lru_lambda, q_norm_g, k_norm_g,
           rel_bias, out_norm_lru_g, out_norm_attn_g, w_out, norm_mlp_g, w_up, w_down, norm_ple_g,
           w_ple_gate, w_ple_proj):
    f = np.float32
    x = np.asarray(x, f)
    p = np.asarray(p, f)
    Bsz, S, _ = x.shape

    def cols(v, n):
        return np.asarray(v, f).reshape(n, 128).T

    par = np.zeros((128, NPAR), f)
    par[:, 0:8] = cols(norm_mix_g[0], 8)
    par[:, 8:16] = cols(norm_mlp_g[0], 8)
    par[:, 16:24] = cols(norm_ple_g[0], 8)
    cw = np.asarray(conv_w[0], f)
    for c in range(4):
        for k in range(4):
            par[:, 24 + c * 4 + k] = cw[k, c * 128:(c + 1) * 128]
    par[:, 40:44] = cols(conv_b[0], 4)
    par[:, 44:48] = cols(b_rg[0], 4)
    par[:, 48:52] = cols(b_ig[0], 4)
    par[:, 52:56] = cols(lru_lambda[0], 4)
    par[:, 56:60] = cols(out_norm_lru_g[0], 4)
    par[:, 60:64] = cols(out_norm_attn_g[0], 4)
    par[:, 64] = np.tile(np.asarray(q_norm_g[0], f), 2)
    par[:, 65] = np.tile(np.asarray(k_norm_g[0], f), 2)

    def blockdiag(w):
        w = np.asarray(w, f)
        o = np.zeros((4, 128, 128), f)
        for n in range(8):
            c, r = n // 2, (n % 2) * 64
            o[c, r:r + 64, r:r + 64] = w[n]
        return o

    wrg = blockdiag(w_rg[0])
    wig = blockdiag(w_ig[0])
    pp_ = np.arange(128)[:, None]
    cc_ = np.arange(640)[None, :]
    idx = np.clip(cc_ - pp_, -256, 256) + 256
    btab = np.ascontiguousarray(np.asarray(rel_bias[0], f)[:, idx].transpose(1, 0, 2))
    dchunk = cc_ // 64 - pp_ // 64
    mask = ((dchunk >= 0) & (dchunk <= 8)).astype(f)
    ident = np.eye(128, dtype=f)
    bones = np.zeros((128, 128), f)
    bones[0:64, 0:64] = 1.0
    bones[64:128, 64:128] = 1.0

    shared = {
        "w_in": np.ascontiguousarray(w_in[0], f), "w_out": np.ascontiguousarray(w_out[0], f),
        "w_up": np.ascontiguousarray(w_up[0], f), "w_down": np.ascontiguousarray(w_down[0], f),
        "w_gate": np.ascontiguousarray(w_ple_gate[0], f), "w_ple": np.ascontiguousarray(w_ple_proj[0], f),
        "wrg": wrg, "wig": wig, "par": par, "btab": btab, "mask": mask, "ident": ident, "bones": bones,
    }
    nc = _get_program(S)
    in_maps = []
    for b in range(Bsz):
        m = dict(shared)
        m["x"] = np.ascontiguousarray(x[b])
        m["p"] = np.ascontiguousarray(p[0, b])
        in_maps.append(m)
    res = run_bass_kernel_spmd(nc, in_maps, core_ids=list(range(Bsz)))
    return np.stack([np.asarray(r["out"], f) for r in res.results], axis=0)
```

```python
import numpy as np
import concourse.bass as bass
import concourse.mybir as mybir
from concourse.bass_utils import run_bass_kernel_spmd

F32 = mybir.dt.float32
BF16 = mybir.dt.bfloat16
AF = mybir.ActivationFunctionType
ALU = mybir.AluOpType

D = 1024
NPAR = 66
EPS = 1e-6
ENGS = ("pe", "act", "dve", "pool", "sp")


class Reg:
    __slots__ = ("name", "excl", "writer", "readers", "last")

    def __init__(self, name, excl=False):
        self.name = name
        self.excl = excl
        self.writer = None
        self.readers = []
        self.last = None


class Chan:
    def __init__(self, sem):
        self.sem = sem
        self.count = 0
        self.last = None


class Ins:
    __slots__ = ("eng", "fn", "deps", "sig", "signals", "chan", "dval", "is_dma")

    def __init__(self, eng, fn):
        self.eng = eng
        self.fn = fn
        self.deps = []
        self.sig = 0
        self.signals = False
        self.chan = None
        self.dval = 0
        self.is_dma = False


class Prog:
    def __init__(self, nc):
        self.nc = nc
        self.q = {e: [] for e in ENGS}
        self.sems = {}
        self.chans = []
        self.regs = {}

    def R(self, name, excl=False):
        r = self.regs.get(name)
        if r is None:
            r = Reg(name, excl)
            self.regs[name] = r
        return r

    def chan(self):
        c = Chan(None)
        self.chans.append(c)
        return c

    def add(self, eng, fn, reads=(), writes=(), chan=None):
        ins = Ins(eng, fn)
        deps = {}

        def dep(d, raw):
            if d is None or d is ins:
                return
            if d.is_dma:
                deps[id(d)] = d
                return
            if d.eng == eng and (eng == "pe" or not raw):
                return
            deps[id(d)] = d

        for r in reads:
            dep(r.writer, True)
            if r.excl:
                dep(r.last, False)
        for w in writes:
            dep(w.writer, True)
            for rd in w.readers:
                dep(rd, False)
            if w.excl:
                dep(w.last, False)
        for r in reads:
            rl = r.readers
            if chan is None and rl and (not rl[-1].is_dma) and rl[-1].eng == eng:
                rl[-1] = ins
            else:
                for k in range(len(rl)):
                    if (not rl[k].is_dma) and rl[k].eng == eng and chan is None:
                        del rl[k]
                        break
                rl.append(ins)
            r.last = ins
        for w in writes:
            w.writer = ins
            w.readers = []
            w.last = ins
        ins.deps = list(deps.values())
        for d in ins.deps:
            d.signals = True
        if chan is not None:
            ins.is_dma = True
            ins.chan = chan
            chan.count += 16
            ins.dval = chan.count
            chan.last = ins
        self.q[eng].append(ins)
        return ins

    def barrier(self, chans):
        lasts = []
        for e in ENGS:
            for ins in reversed(self.q[e]):
                if not ins.is_dma and ins.fn is not None:
                    lasts.append(ins)
                    break
        dl = [c.last for c in chans if c.last is not None]
        for e in ENGS:
            ins = Ins(e, None)
            ins.deps = [d for d in lasts if d.eng != e] + dl
            for d in ins.deps:
                d.signals = True
            self.q[e].append(ins)

    def emit(self, es, final_chans):
        nc = self.nc
        for e in ENGS:
            self.sems[e] = es.enter_context(nc.semaphore("s_" + e))
        for k, c in enumerate(self.chans):
            c.sem = es.enter_context(nc.semaphore("c%d" % k))
        for e in ENGS:
            n = 0
            for ins in self.q[e]:
                if ins.signals and not ins.is_dma:
                    n += 1
                    ins.sig = n
        engobj = {"pe": "tensor", "act": "scalar", "dve": "vector", "pool": "gpsimd", "sp": "sync"}
        with nc.Block() as block:
            for e in ENGS:
                def body(eng, e=e):
                    seen = {}
                    for ins in self.q[e]:
                        for d in ins.deps:
                            if d.is_dma:
                                sem, val = d.chan.sem, d.dval
                            else:
                                sem, val = self.sems[d.eng], d.sig
                            k = id(sem)
                            if seen.get(k, 0) >= val:
                                continue
                            seen[k] = val
                            eng.wait_ge(sem, val)
                        if ins.fn is None:
                            continue
                        bi = ins.fn(eng)
                        if ins.is_dma:
                            bi.then_inc(ins.chan.sem, 16)
                        elif ins.signals:
                            bi.then_inc(self.sems[e], 1)
                    if e == "sp":
                        for ch in final_chans:
                            eng.wait_ge(ch.sem, ch.count)
                getattr(block, engobj[e])(body)


def MM(out, lhsT, rhs, start, stop):
    return lambda e: e.matmul(out, lhsT, rhs, start=start, stop=stop)


def TR(out, in_, ident):
    return lambda e: e.transpose(out, in_, ident)


def ACT(out, in_, func, bias=None, scale=1.0, accum=None):
    def f(e):
        kw = {}
        if bias is not None:
            kw["bias"] = bias
        if accum is not None:
            kw["accum_out"] = accum
        return e.activation(out=out, in_=in_, func=func, scale=scale, **kw)
    return f


def TT(out, a, b, op):
    return lambda e: e.tensor_tensor(out, a, b, op)


def TS(out, a, s1, s2, op0, op1):
    return lambda e: e.tensor_scalar(out, a, s1, s2, op0, op1)


def TSM(out, a, s):
    return lambda e: e.tensor_scalar_mul(out, a, s)


def STT(out, a, s, b, op0, op1):
    return lambda e: e.scalar_tensor_tensor(out, a, s, b, op0, op1)


def CP(out, a):
    return lambda e: e.tensor_copy(out, a)


def MEMSET(ap, v):
    return lambda e: e.memset(ap, v)


def RECIP(out, a):
    return lambda e: e.reciprocal(out, a)


def SCAN(out, d0, d1, init, op0, op1):
    return lambda e: e.tensor_tensor_scan(out, d0, d1, init, op0, op1)


def DMA(out, in_):
    return lambda e: e.dma_start(out=out, in_=in_)


class Arena:
    def __init__(self, nc, base, limit):
        self.nc = nc
        self.off = base
        self.limit = limit
        self.n = 0

    def alloc(self, name, shape, dt):
        esz = 2 if dt == BF16 else 4
        size = esz
        for s in shape[1:]:
            size *= s
        size = (size + 63) // 64 * 64
        assert self.off + size <= self.limit, ("SBUF overflow", name, self.off, size, self.limit)
        self.n += 1
        t = self.nc.alloc_sbuf_tensor_at("%s_%d" % (name, self.n), list(shape), dt, offset=self.off)
        self.off += size
        return t


def build_program(S):
    from contextlib import ExitStack
    NT = S // 512
    NTB = S // 256
    nc = bass.Bass("TRN2", target_bir_lowering=False)

    def din(name, shape):
        return nc.dram_tensor(name, list(shape), F32, kind="ExternalInput").ap()

    x_d = din("x", [S, D])
    p_d = din("p", [S, 256])
    win_d = din("w_in", [D, 2560])
    wout_d = din("w_out", [D, D])
    wup_d = din("w_up", [D, 4096])
    wdn_d = din("w_down", [4096, D])
    wg_d = din("w_gate", [D, D])
    wp_d = din("w_ple", [256, D])
    wrg_d = din("wrg", [4, 128, 128])
    wig_d = din("wig", [4, 128, 128])
    par_d = din("par", [128, NPAR])
    btab_d = din("btab", [128, 8, 640])
    mask_d = din("mask", [128, 640])
    ident_d = din("ident", [128, 128])
    bones_d = din("bones", [128, 128])
    out_d = nc.dram_tensor("out", [S, D], F32, kind="ExternalOutput").ap()
    h1_d = nc.dram_tensor("h1s", [S, D], F32, kind="Internal").ap()

    P = Prog(nc)
    R = P.R
    base = (nc._sbuf_addr_for_side("left") + 63) // 64 * 64
    limit = nc._sbuf_addr_for_side("right")
    A0 = Arena(nc, base, limit)

    par = A0.alloc("par", [128, NPAR], F32)
    ident = A0.alloc("ident", [128, 128], BF16)
    bones = A0.alloc("bones", [128, 128], BF16)
    onesc = A0.alloc("onesc", [128, 2], BF16)
    sca = A0.alloc("sca", [128, 4], F32)
    gg = A0.alloc("gg", [128, 1], F32)
    carry = A0.alloc("carry", [128, 4], F32)
    ss1 = A0.alloc("ss1", [128, 4], F32)
    rs1 = A0.alloc("rs1", [128, 4], F32)
    ssA = A0.alloc("ssA", [128, 4], F32)
    rsA = A0.alloc("rsA", [128, 4], F32)
    rsL = A0.alloc("rsL", [128, 4], F32)
    rden = A0.alloc("rden", [128, 4], F32)
    junk = A0.alloc("junk", [128, 1024], BF16)
    XS = [A0.alloc("xs", [128, 1024], BF16) for _ in range(2)]
    common_end = A0.off

    banks = [nc.alloc_psum_tensor("bank%d" % b, [128, 512], F32) for b in range(8)]
    PB = [R("bank%d" % b, True) for b in range(8)]

    def bf(b):
        return banks[b][:]

    def bb(b):
        return banks[b][:].bitcast(BF16)

    rot_state = [0]

    def rot():
        b = rot_state[0]
        rot_state[0] = (b + 1) % 4
        return b

    A = Arena(nc, common_end, limit)
    Win = A.alloc("Win", [128, 8, 2560], BF16)
    Wout = A.alloc("Wout", [128, 8, 1024], BF16)
    Wrg = A.alloc("Wrg", [128, 4, 128], BF16)
    Wig = A.alloc("Wig", [128, 4, 128], BF16)
    XH = [A.alloc("xh", [128, 4, 1024], F32) for _ in range(2)]
    uT = A.alloc("uT", [128, 8, 512], BF16)
    xle = A.alloc("xle", [128, 4, 516], F32)
    xc = [A.alloc("xc", [128, 512], F32) for _ in range(2)]
    xcb = [A.alloc("xcb", [128, 512], BF16) for _ in range(2)]
    rr0 = A.alloc("rr", [128, 512], F32)
    rr = [rr0, rr0]
    ig = [A.alloc("ig", [128, 512], F32) for _ in range(2)]
    aa = [A.alloc("aa", [128, 512], F32) for _ in range(2)]
    t1 = [A.alloc("t1", [128, 512], F32) for _ in range(2)]
    hh = [A.alloc("hh", [128, 512], F32) for _ in range(2)]
    gl = [A.alloc("gl", [128, 512], F32) for _ in range(2)]
    sqy = A.alloc("sqy", [128, 4, 512], BF16)
    mst_off = A.off
    qraw = [A.alloc("qraw", [128, 512], F32) for _ in range(2)]
    sqq = [A.alloc("sqq", [128, 512], BF16) for _ in range(2)]
    rsq = [A.alloc("rsq", [128, 512], F32) for _ in range(2)]
    qT = A.alloc("qT", [128, 4, 512], BF16)
    kT = A.alloc("kT", [128, 4, 1024], BF16)
    Vr = A.alloc("Vr", [128, 8, 8, 65], BF16)
    E = A.alloc("E", [128, 8, 640], BF16)
    stage_off = A.off
    Pb = [A.alloc("Pb", [128, 2560], BF16) for _ in range(2)]
    ygT = A.alloc("ygT", [128, 8, 512], BF16)
    yat = A.alloc("yat", [128, 4, 512], BF16)
    bst = nc.alloc_sbuf_tensor_at("bstage", [128, 8, 640], F32, offset=stage_off)
    mst = nc.alloc_sbuf_tensor_at("mstage", [128, 640], F32, offset=mst_off)
    assert stage_off + 8 * 640 * 4 <= A.off

    B = Arena(nc, common_end, limit)
    Wup = B.alloc("Wup", [128, 8, 4096], BF16)
    Wdn = B.alloc("Wdn", [128, 32, 1024], BF16)
    Wg = B.alloc("Wg", [128, 8, 1024], BF16)
    Wp = B.alloc("Wp", [128, 2, 1024], BF16)
    HB = [B.alloc("hb", [128, 2, 1024], F32) for _ in range(2)]
    PBF = [B.alloc("pb", [128, 2, 256], F32) for _ in range(2)]
    u2T = B.alloc("u2T", [128, 8, 256], BF16)
    rl = [B.alloc("rl", [128, 256], BF16) for _ in range(4)]
    hid = [B.alloc("hid", [128, 256], BF16) for _ in range(4)]
    gt = [B.alloc("gt", [128, 512], F32) for _ in range(2)]
    tmp = [B.alloc("tmp", [128, 512], F32) for _ in range(2)]
    pbb = [B.alloc("pbb", [128, 256], BF16) for _ in range(2)]
    pT = B.alloc("pT", [128, 2, 256], BF16)
    ssB = B.alloc("ssB", [128, 2], F32)
    rsB = B.alloc("rsB", [128, 2], F32)

    c_par = P.chan()
    c_small = P.chan()
    c_stage = P.chan()
    c_stage2 = P.chan()
    P.add("sp", DMA(par[:], par_d[:]), writes=[R("par")], chan=c_par)
    P.add("sp", DMA(bst[:], btab_d[:]), writes=[R("bst")], chan=c_stage)
    P.add("sp", DMA(mst[:], mask_d[:]), writes=[R("mst")], chan=c_stage2)
    P.add("pool", DMA(ident[:], ident_d[:]), writes=[R("ident")], chan=c_small)
    P.add("pool", DMA(bones[:], bones_d[:]), writes=[R("bones")], chan=c_small)
    P.add("pool", DMA(Wrg[:], wrg_d.rearrange("c p n -> p c n")), writes=[R("wrg")], chan=c_small)
    P.add("pool", DMA(Wig[:], wig_d.rearrange("c p n -> p c n")), writes=[R("wig")], chan=c_small)
    c_x = [P.chan(), P.chan()]

    def xh_regs(buf):
        return [R("xh%d_%d" % (buf, tg)) for tg in range(4)]

    def load_x(i):
        buf = i % 2
        src = x_d[i * 512:(i + 1) * 512, :].rearrange("(g p) f -> p g f", p=128)
        P.add("sp", DMA(XH[buf][:], src), writes=xh_regs(buf), chan=c_x[buf])

    load_x(0)
    win_v = win_d.rearrange("(k p) n -> p k n", p=128)
    c_win = []
    for pc in range(5):
        c = P.chan()
        c_win.append(c)
        P.add("pool", DMA(Win[:, :, pc * 512:(pc + 1) * 512], win_v[:, :, pc * 512:(pc + 1) * 512]),
              writes=[R("win%d" % pc)], chan=c)
    c_wout = P.chan()
    P.add("pool", DMA(Wout[:], wout_d.rearrange("(k p) n -> p k n", p=128)), writes=[R("wout")], chan=c_wout)

    P.add("dve", MEMSET(carry[:], 0.0), writes=[R("carry")])
    P.add("dve", MEMSET(xle[:], 0.0), writes=[R("xle%d" % c) for c in range(4)])
    P.add("dve", MEMSET(onesc[:], 1.0), writes=[R("onesc")])
    P.add("dve", MEMSET(Vr[:], 1.0), writes=[R("V%d" % s) for s in range(8)])
    P.add("act", ACT(sca[:], par[:, 52:56], AF.Exp, scale=-1.0), reads=[R("par")], writes=[R("sca")])
    P.add("act", ACT(sca[:], sca[:], AF.Ln, bias=1.0), reads=[R("sca")], writes=[R("sca")])
    P.add("dve", TSM(sca[:], sca[:], -8.0), reads=[R("sca")], writes=[R("sca")])
    P.add("dve", STT(gg[:], par[:, 64:65], 0.125, par[:, 65:66], ALU.mult, ALU.mult),
          reads=[R("par")], writes=[R("gg")])
    for h in range(8):
        P.add("act", ACT(bst[:, h, :], bst[:, h, :], AF.Exp), reads=[R("bst")], writes=[R("bst")])
        P.add("dve", TT(E[:, h, :], bst[:, h, :], mst[:], ALU.mult), reads=[R("bst"), R("mst")],
              writes=[R("E")])
    P.barrier([c_par, c_small, c_stage, c_stage2])

    c_h1s = [P.chan(), P.chan()]
    widths = []
    for b in range(8):
        ql = max(0, 2 * b - 8)
        qh = min(7, 2 * b + 1)
        widths.append((ql, (qh - ql + 1) * 64))
    poff = [0]
    for b in range(8):
        poff.append(poff[-1] + widths[b][1])

    def inproj_ws(col0, bank):
        pc = col0 // 512
        for k in range(8):
            P.add("pe", MM(bf(bank), Win[:, k, col0:col0 + 128], uT[:, k, :], k == 0, k == 7),
                  reads=[R("uT%d" % k), R("win%d" % pc)], writes=[PB[bank]])

    GC = 1.5957691216057308

    def norm1(i):
        cur = i % 2
        xh = XH[cur]
        xr = xh_regs(cur)
        P.add("dve", MEMSET(ss1[:], 0.0), writes=[R("ss1")])
        for tg in range(4):
            P.add("act", ACT(junk[:], xh[:, tg, :], AF.Square, accum=ss1[:, tg:tg + 1]),
                  reads=[xr[tg]], writes=[R("junk"), R("ss1")])
        P.add("act", ACT(rs1[:], ss1[:], AF.Ln, bias=EPS, scale=1.0 / D), reads=[R("ss1")], writes=[R("rs1")])
        P.add("act", ACT(rs1[:], rs1[:], AF.Exp, scale=-0.5), reads=[R("rs1")], writes=[R("rs1")])
        for tg in range(4):
            xs = XS[tg % 2]
            xsr = R("xs%d" % (tg % 2))
            P.add("dve", TSM(xs[:], xh[:, tg, :], rs1[:, tg:tg + 1]), reads=[xr[tg], R("rs1")], writes=[xsr])
            for c in range(8):
                bank = 4 + c // 2
                o = (c % 2) * 512 + tg * 128
                P.add("pe", TR(bb(bank)[:, o:o + 128], xs[:, c * 128:(c + 1) * 128], ident[:]),
                      reads=[xsr, R("ident")], writes=[PB[bank]])
        for c in range(8):
            bank = 4 + c // 2
            o = (c % 2) * 512
            P.add("dve", TSM(uT[:, c, :], bb(bank)[:, o:o + 512], par[:, c:c + 1]),
                  reads=[PB[bank], R("par")], writes=[R("uT%d" % c)])

    def mixer(i):
        cur = i % 2
        prv = 1 - cur

        def lru_F(c):
            s = c % 2
            b1 = rot()
            inproj_ws(c * 128, b1)
            P.add("act", ACT(xle[:, c, 3:515], bf(b1), AF.Copy), reads=[PB[b1]], writes=[R("xle%d" % c)])
            b2 = rot()
            inproj_ws(512 + c * 128, b2)
            P.add("act", ACT(gl[s][:], bf(b2), AF.Copy), reads=[PB[b2]], writes=[R("gl%d" % s)])
            rx = [R("xle%d" % c), R("par")]
            P.add("pool", TS(xc[s][:], xle[:, c, 0:512], par[:, 24 + c * 4:25 + c * 4], par[:, 40 + c:41 + c],
                             ALU.mult, ALU.add), reads=rx, writes=[R("xc%d" % s)])
            for k in range(1, 4):
                P.add("dve", STT(xc[s][:], xle[:, c, k:k + 512], par[:, 24 + c * 4 + k:25 + c * 4 + k], xc[s][:],
                                 ALU.mult, ALU.add), reads=rx + [R("xc%d" % s)], writes=[R("xc%d" % s)])
            P.add("pool", CP(xle[:, c, 0:3], xle[:, c, 512:515]), reads=[R("xle%d" % c)], writes=[R("xle%d" % c)])
            P.add("pool", CP(xcb[s][:], xc[s][:]), reads=[R("xc%d" % s)], writes=[R("xcb%d" % s)])
            P.add("pool", TT(t1[s][:], gl[s][:], gl[s][:], ALU.mult), reads=[R("gl%d" % s)], writes=[R("t1%d" % s)])
            P.add("pool", TS(t1[s][:], t1[s][:], 0.044715, 1.0, ALU.mult, ALU.add), reads=[R("t1%d" % s)],
                  writes=[R("t1%d" % s)])
            P.add("pool", TT(t1[s][:], t1[s][:], gl[s][:], ALU.mult), reads=[R("t1%d" % s), R("gl%d" % s)],
                  writes=[R("t1%d" % s)])

        def lru_S(c):
            s = c % 2
            br = rot()
            P.add("pe", MM(bf(br), Wrg[:, c, :], xcb[s][:], True, True), reads=[R("xcb%d" % s), R("wrg")],
                  writes=[PB[br]])
            P.add("act", ACT(aa[s][:], bf(br), AF.Sigmoid, bias=par[:, 44 + c:45 + c]), reads=[PB[br], R("par")],
                  writes=[R("aa%d" % s)])
            bi = rot()
            P.add("pe", MM(bf(bi), Wig[:, c, :], xcb[s][:], True, True), reads=[R("xcb%d" % s), R("wig")],
                  writes=[PB[bi]])
            P.add("act", ACT(ig[s][:], bf(bi), AF.Sigmoid, bias=par[:, 48 + c:49 + c]), reads=[PB[bi], R("par")],
                  writes=[R("ig%d" % s)])
            P.add("act", ACT(t1[s][:], t1[s][:], AF.Sigmoid, scale=GC), reads=[R("t1%d" % s)], writes=[R("t1%d" % s)])
            P.add("dve", TT(gl[s][:], gl[s][:], t1[s][:], ALU.mult), reads=[R("gl%d" % s), R("t1%d" % s)],
                  writes=[R("gl%d" % s)])

        def lru_E(c):
            s = c % 2
            P.add("act", ACT(aa[s][:], aa[s][:], AF.Exp, scale=sca[:, c:c + 1]), reads=[R("aa%d" % s), R("sca")],
                  writes=[R("aa%d" % s)])
            P.add("dve", STT(t1[s][:], aa[s][:], -1.0, aa[s][:], ALU.mult, ALU.mult), reads=[R("aa%d" % s)],
                  writes=[R("t1%d" % s)])
            P.add("dve", TS(t1[s][:], t1[s][:], 1.0, 1e-12, ALU.add, ALU.max), reads=[R("t1%d" % s)],
                  writes=[R("t1%d" % s)])
            P.add("act", ACT(t1[s][:], t1[s][:], AF.Ln), reads=[R("t1%d" % s)], writes=[R("t1%d" % s)])
            P.add("act", ACT(t1[s][:], t1[s][:], AF.Exp, scale=0.5), reads=[R("t1%d" % s)], writes=[R("t1%d" % s)])
            P.add("dve", TT(ig[s][:], ig[s][:], xc[s][:], ALU.mult), reads=[R("ig%d" % s), R("xc%d" % s)],
                  writes=[R("ig%d" % s)])
            P.add("dve", TT(ig[s][:], ig[s][:], t1[s][:], ALU.mult), reads=[R("ig%d" % s), R("t1%d" % s)],
                  writes=[R("ig%d" % s)])
            P.add("dve", SCAN(hh[s][:], aa[s][:], ig[s][:], carry[:, c:c + 1], ALU.mult, ALU.add),
                  reads=[R("aa%d" % s), R("ig%d" % s), R("carry")], writes=[R("hh%d" % s)])
            P.add("dve", CP(carry[:, c:c + 1], hh[s][:, 511:512]), reads=[R("hh%d" % s)], writes=[R("carry")])
            P.add("dve", TT(gl[s][:], gl[s][:], hh[s][:], ALU.mult), reads=[R("gl%d" % s), R("hh%d" % s)],
                  writes=[R("gl%d" % s)])
            P.add("act", ACT(sqy[:, c, :], gl[s][:], AF.Square), reads=[R("gl%d" % s)], writes=[R("sqy%d" % c)])
            P.add("dve", TSM(ygT[:, c, :], gl[s][:], par[:, 56 + c:57 + c]), reads=[R("gl%d" % s), R("par")],
                  writes=[R("ygT%d" % c)])

        def qk_front(n):
            s = n % 2
            col0 = 1024 + n * 128
            b = rot()
            inproj_ws(col0, b)
            P.add("dve", CP(qraw[s][:], bf(b)), reads=[PB[b]], writes=[R("qraw%d" % s)])
            P.add("act", ACT(sqq[s][:], qraw[s][:], AF.Square), reads=[R("qraw%d" % s)], writes=[R("sqq%d" % s)])

        def qk_back(n):
            s = n % 2
            b2 = rot()
            P.add("pe", MM(bf(b2), bones[:], sqq[s][:], True, True), reads=[R("sqq%d" % s), R("bones")],
                  writes=[PB[b2]])
            P.add("act", ACT(rsq[s][:], bf(b2), AF.Ln, bias=EPS, scale=1.0 / 64), reads=[PB[b2]],
                  writes=[R("rsq%d" % s)])
            P.add("act", ACT(rsq[s][:], rsq[s][:], AF.Exp, scale=-0.5), reads=[R("rsq%d" % s)],
                  writes=[R("rsq%d" % s)])
            if n < 4:
                P.add("dve", STT(qT[:, n, :], qraw[s][:], gg[:, 0:1], rsq[s][:], ALU.mult, ALU.mult),
                      reads=[R("qraw%d" % s), R("gg"), R("rsq%d" % s)], writes=[R("qT%d" % n)])
            else:
                j = n - 4
                P.add("dve", TT(kT[:, j, cur * 512:(cur + 1) * 512], qraw[s][:], rsq[s][:], ALU.mult),
                      reads=[R("qraw%d" % s), R("rsq%d" % s)], writes=[R("kT%d_%d" % (j, cur))])

        lru_F(0)
        lru_F(1)
        lru_S(0)
        lru_S(1)
        qk_front(0)
        lru_E(0)
        qk_front(1)
        qk_back(0)
        lru_E(1)
        qk_front(2)
        qk_back(1)
        qk_front(3)
        qk_back(2)
        lru_F(2)
        qk_back(3)
        lru_F(3)
        lru_S(2)
        lru_S(3)
        qk_front(4)
        lru_E(2)
        qk_front(5)
        qk_back(4)
        lru_E(3)
        qk_front(6)
        qk_back(5)
        qk_front(7)
        qk_back(6)
        for tg in range(4):
            b = rot()
            for k in range(8):
                P.add("pe", MM(bf(b), uT[:, k, tg * 128:(tg + 1) * 128], Win[:, k, 2048:2560], k == 0, k == 7),
                      reads=[R("uT%d" % k), R("win4")], writes=[PB[b]])
            slot = 4 * cur + tg
            P.add("act", ACT(Vr[:, slot, :, 0:64], bf(b).rearrange("p (h d) -> p h d", h=8), AF.Copy),
                  reads=[PB[b]], writes=[R("V%d" % slot)])
            if tg == 0:
                qk_back(7)

        bs = rot()
        for tg in range(4):
            for c in range(4):
                P.add("pe", MM(bf(bs)[:, tg:tg + 1], sqy[:, c, tg * 128:(tg + 1) * 128], onesc[:, 0:1], c == 0, c == 3),
                      reads=[R("sqy%d" % c), R("onesc")], writes=[PB[bs]])
        P.add("act", ACT(rsL[:], bf(bs)[:, 0:4], AF.Ln, bias=EPS, scale=1.0 / 512), reads=[PB[bs]], writes=[R("rsL")])
        P.add("act", ACT(rsL[:], rsL[:], AF.Exp, scale=-0.5), reads=[R("rsL")], writes=[R("rsL")])

        blocks = [b for b in range(8) if (i > 0 or b >= 4)]
        for h in range(8):
            j = h // 2
            ro = 64 * (h % 2)
            st = h % 2
            hq = h % 4
            for b in blocks:
                ql, w = widths[b]
                half = prv if b < 4 else cur
                kcol = half * 512 + (b % 4) * 128
                bank = rot()
                P.add("pe", MM(bf(bank)[:, 0:w], kT[ro:ro + 64, j, kcol:kcol + 128],
                               qT[ro:ro + 64, j, ql * 64:ql * 64 + w], True, True),
                      reads=[R("kT%d_%d" % (j, half)), R("qT%d" % j)], writes=[PB[bank]])
                pr = R("P%d_%d" % (st, b))
                pv = Pb[st][:, poff[b]:poff[b] + w]
                P.add("act", ACT(pv, bf(bank)[:, 0:w], AF.Exp), reads=[PB[bank]], writes=[pr])
                ec = (ql + 8 - 2 * b) * 64
                P.add("dve", TT(pv, pv, E[:, h, ec:ec + w], ALU.mult), reads=[pr, R("E")], writes=[pr])
            for g in range(4):
                bl = [b for b in range(g, g + 5) if b in blocks]
                for n, b in enumerate(bl):
                    ql, w = widths[b]
                    half = prv if b < 4 else cur
                    slot = 4 * half + (b % 4)
                    pcn = poff[b] + (2 * g - ql) * 64
                    P.add("pe", MM(bf(4 + g)[:, hq * 65:(hq + 1) * 65], Pb[st][:, pcn:pcn + 128], Vr[:, slot, h, :],
                                   n == 0, n == len(bl) - 1),
                          reads=[R("P%d_%d" % (st, b)), R("V%d" % slot)], writes=[PB[4 + g]])
            if hq == 3:
                hb0 = (h // 4) * 4
                for g in range(4):
                    v3 = bf(4 + g)[:, 0:260].rearrange("p (h e) -> p h e", e=65)
                    P.add("dve", RECIP(rden[:], v3[:, :, 64]), reads=[PB[4 + g]], writes=[R("rden")])
                    P.add("dve", TT(yat[:, g, hb0 * 64:(hb0 + 4) * 64].rearrange("p (h d) -> p h d", d=64),
                                    v3[:, :, 0:64], rden[:].unsqueeze(2).to_broadcast([128, 4, 64]), ALU.mult),
                          reads=[PB[4 + g], R("rden")], writes=[R("yat%d" % g)])
        P.add("dve", MEMSET(ssA[:], 0.0), writes=[R("ssA")])
        for g in range(4):
            P.add("act", ACT(junk[:, 0:512], yat[:, g, :], AF.Square, accum=ssA[:, g:g + 1]),
                  reads=[R("yat%d" % g)], writes=[R("junk"), R("ssA")])
        P.add("act", ACT(rsA[:], ssA[:], AF.Ln, bias=EPS, scale=1.0 / 512), reads=[R("ssA")], writes=[R("rsA")])
        P.add("act", ACT(rsA[:], rsA[:], AF.Exp, scale=-0.5), reads=[R("rsA")], writes=[R("rsA")])
        for fc in range(4):
            bank = 4 + fc // 2
            for g in range(4):
                o = (fc % 2) * 512 + g * 128
                P.add("pe", TR(bb(bank)[:, o:o + 128], yat[:, g, fc * 128:(fc + 1) * 128], ident[:]),
                      reads=[R("yat%d" % g), R("ident")], writes=[PB[bank]])
        for fc in range(4):
            bank = 4 + fc // 2
            o = (fc % 2) * 512
            P.add("dve", TSM(ygT[:, 4 + fc, :], bb(bank)[:, o:o + 512], par[:, 60 + fc:61 + fc]),
                  reads=[PB[bank], R("par")], writes=[R("ygT%d" % (4 + fc))])

    def outproj(i):
        cur = i % 2
        xh = XH[cur]
        xr = xh_regs(cur)
        for tg in range(4):
            for half in range(2):
                hv = xh[:, tg, half * 512:(half + 1) * 512]
                bA = rot()
                for k in range(4):
                    P.add("pe", MM(bf(bA), ygT[:, k, tg * 128:(tg + 1) * 128], Wout[:, k, half * 512:(half + 1) * 512],
                                   k == 0, k == 3), reads=[R("ygT%d" % k), R("wout")], writes=[PB[bA]])
                bB = rot()
                for k in range(4, 8):
                    P.add("pe", MM(bf(bB), ygT[:, k, tg * 128:(tg + 1) * 128], Wout[:, k, half * 512:(half + 1) * 512],
                                   k == 4, k == 7), reads=[R("ygT%d" % k), R("wout")], writes=[PB[bB]])
                P.add("dve", STT(hv, bf(bA), rsL[:, tg:tg + 1], hv, ALU.mult, ALU.add),
                      reads=[PB[bA], R("rsL"), xr[tg]], writes=[xr[tg]])
                P.add("dve", STT(hv, bf(bB), rsA[:, tg:tg + 1], hv, ALU.mult, ALU.add),
                      reads=[PB[bB], R("rsA"), xr[tg]], writes=[xr[tg]])
        dst = h1_d[i * 512:(i + 1) * 512, :].rearrange("(g p) f -> p g f", p=128)
        P.add("sp", DMA(dst, xh[:]), reads=xr, writes=[R("h1_%d" % i)], chan=c_h1s[cur])

    norm1(0)
    for i in range(NT):
        if i + 1 < NT:
            load_x(i + 1)
        mixer(i)
        if i + 1 < NT:
            norm1(i + 1)
        outproj(i)

    P.barrier(c_h1s + c_x + c_win + [c_wout])

    c_hb = [P.chan(), P.chan()]
    c_pb = [P.chan(), P.chan()]
    c_out = [P.chan(), P.chan()]
    wup_v = wup_d.rearrange("(k p) n -> p k n", p=128)
    wdn_v = wdn_d.rearrange("(f p) n -> p f n", p=128)
    c_w = []

    def hb_regs(buf):
        return [R("hb%d_%d" % (buf, tg)) for tg in range(2)]

    def load_b(n):
        buf = n % 2
        src = h1_d[n * 256:(n + 1) * 256, :].rearrange("(g p) f -> p g f", p=128)
        P.add("sp", DMA(HB[buf][:], src), reads=[R("h1_%d" % (n // 2))], writes=hb_regs(buf), chan=c_hb[buf])
        srcp = p_d[n * 256:(n + 1) * 256, :].rearrange("(g p) f -> p g f", p=128)
        P.add("sp", DMA(PBF[buf][:], srcp), writes=[R("pb%d" % buf)], chan=c_pb[buf])

    load_b(0)
    for pc in range(8):
        c = P.chan()
        c_w.append(c)
        P.add("pool", DMA(Wup[:, :, pc * 512:(pc + 1) * 512], wup_v[:, :, pc * 512:(pc + 1) * 512]),
              writes=[R("wup%d" % pc)], chan=c)
        c = P.chan()
        c_w.append(c)
        P.add("pool", DMA(Wdn[:, pc * 4:(pc + 1) * 4, :], wdn_v[:, pc * 4:(pc + 1) * 4, :]),
              writes=[R("wdn%d" % pc)], chan=c)
    c = P.chan()
    c_w.append(c)
    P.add("pool", DMA(Wg[:], wg_d.rearrange("(k p) n -> p k n", p=128)), writes=[R("wg")], chan=c)
    c = P.chan()
    c_w.append(c)
    P.add("pool", DMA(Wp[:], wp_d.rearrange("(k p) n -> p k n", p=128)), writes=[R("wp")], chan=c)

    u2Tb = [u2T, B.alloc("u2Tb", [128, 8, 256], BF16)]
    u3T = B.alloc("u3T", [128, 8, 256], BF16)

    def norm_b1(hb, hr):
        P.add("dve", MEMSET(ssB[:], 0.0), writes=[R("ssB")])
        for tg in range(2):
            P.add("act", ACT(junk[:], hb[:, tg, :], AF.Square, accum=ssB[:, tg:tg + 1]),
                  reads=[hr[tg]], writes=[R("junk"), R("ssB")])
        P.add("act", ACT(rsB[:], ssB[:], AF.Ln, bias=EPS, scale=1.0 / D), reads=[R("ssB")], writes=[R("rsB")])
        P.add("act", ACT(rsB[:], rsB[:], AF.Exp, scale=-0.5), reads=[R("rsB")], writes=[R("rsB")])
        for tg in range(2):
            P.add("dve", TSM(XS[tg][:], hb[:, tg, :], rsB[:, tg:tg + 1]), reads=[hr[tg], R("rsB")],
                  writes=[R("xs%d" % tg)])

    def norm_b2(dstT, dreg, gcol):
        bT = [rot(), rot()]
        for tg in range(2):
            for c in range(8):
                bank = bT[c // 4]
                o = (c % 4) * 256 + tg * 128
                P.add("pe", TR(bb(bank)[:, o:o + 128], XS[tg][:, c * 128:(c + 1) * 128], ident[:]),
                      reads=[R("xs%d" % tg), R("ident")], writes=[PB[bank]])
        for c in range(8):
            bank = bT[c // 4]
            o = (c % 4) * 256
            P.add("dve", TSM(dstT[:, c, :], bb(bank)[:, o:o + 256], par[:, gcol + c:gcol + c + 1]),
                  reads=[PB[bank], R("par")], writes=[dreg])

    def ple_a(n):
        cur = n % 2
        pbf = PBF[cur]
        bt = rot()
        for tg in range(2):
            P.add("dve", CP(pbb[tg][:], pbf[:, tg, :]), reads=[R("pb%d" % cur)], writes=[R("pbb%d" % tg)])
            for kc in range(2):
                o = kc * 256 + tg * 128
                P.add("pe", TR(bb(bt)[:, o:o + 128], pbb[tg][:, kc * 128:(kc + 1) * 128], ident[:]),
                      reads=[R("pbb%d" % tg), R("ident")], writes=[PB[bt]])
        P.add("act", ACT(pT[:].rearrange("p k t -> p (k t)"), bb(bt)[:, 0:512], AF.Copy), reads=[PB[bt]],
              writes=[R("pT")])

    def ple_b(n, tg, half):
        cur = n % 2
        hb = HB[cur]
        hr = hb_regs(cur)
        s = (tg * 2 + half) % 2
        hv = hb[:, tg, half * 512:(half + 1) * 512]
        bgt = rot()
        for k in range(8):
            P.add("pe", MM(bf(bgt), u3T[:, k, tg * 128:(tg + 1) * 128], Wg[:, k, half * 512:(half + 1) * 512],
                           k == 0, k == 7), reads=[R("u3T"), R("wg")], writes=[PB[bgt]])
        P.add("act", ACT(gt[s][:], bf(bgt), AF.Sigmoid), reads=[PB[bgt]], writes=[R("gt%d" % s)])
        bpp = rot()
        for kc in range(2):
            P.add("pe", MM(bf(bpp), pT[:, kc, tg * 128:(tg + 1) * 128], Wp[:, kc, half * 512:(half + 1) * 512],
                           kc == 0, kc == 1), reads=[R("pT"), R("wp")], writes=[PB[bpp]])
        P.add("dve", TT(tmp[s][:], bf(bpp), gt[s][:], ALU.mult), reads=[R("gt%d" % s), PB[bpp]],
              writes=[R("tmp%d" % s)])
        P.add("dve", TT(hv, hv, tmp[s][:], ALU.add), reads=[hr[tg], R("tmp%d" % s)], writes=[hr[tg]])

    def store_b(n):
        cur = n % 2
        dst = out_d[n * 256:(n + 1) * 256, :].rearrange("(g p) f -> p g f", p=128)
        P.add("sp", DMA(dst, HB[cur][:]), reads=hb_regs(cur), chan=c_out[cur])

    def ple_hooks(n):
        hb = HB[n % 2]
        hr = hb_regs(n % 2)
        return {
            1: lambda: norm_b1(hb, hr),
            4: lambda: (norm_b2(u3T, R("u3T"), 16), ple_a(n)),
            8: lambda: ple_b(n, 0, 0),
            10: lambda: ple_b(n, 0, 1),
            12: lambda: ple_b(n, 1, 0),
            14: lambda: (ple_b(n, 1, 1), store_b(n)),
        }

    def mlp(n, hooks):
        cur = n % 2
        hb = HB[cur]
        hr = hb_regs(cur)
        uu = u2Tb[cur]
        ur = R("u2T%d" % cur)

        def up(f):
            b = rot()
            for k in range(8):
                P.add("pe", MM(bf(b)[:, 0:256], Wup[:, k, f * 128:(f + 1) * 128], uu[:, k, :], k == 0, k == 7),
                      reads=[ur, R("wup%d" % (f // 4))], writes=[PB[b]])
            P.add("act", ACT(rl[f % 4][:], bf(b)[:, 0:256], AF.Relu), reads=[PB[b]], writes=[R("rl%d" % (f % 4))])
            P.add("dve", TT(hid[f % 4][:], rl[f % 4][:], rl[f % 4][:], ALU.mult), reads=[R("rl%d" % (f % 4))],
                  writes=[R("hid%d" % (f % 4))])

        def down(f):
            for tg in range(2):
                for half in range(2):
                    bk = 4 + tg * 2 + half
                    P.add("pe", MM(bf(bk), hid[f % 4][:, tg * 128:(tg + 1) * 128],
                                   Wdn[:, f, half * 512:(half + 1) * 512], f == 0, f == 31),
                          reads=[R("hid%d" % (f % 4)), R("wdn%d" % (f // 4))], writes=[PB[bk]])

        up(0)
        up(1)
        for f in range(32):
            if f in hooks:
                hooks[f]()
            if f + 2 < 32:
                up(f + 2)
            down(f)
        for tg in range(2):
            for half in range(2):
                bk = 4 + tg * 2 + half
                hv = hb[:, tg, half * 512:(half + 1) * 512]
                P.add("dve", TT(hv, bf(bk), hv, ALU.add), reads=[hr[tg], PB[bk]], writes=[hr[tg]])

    norm_b1(HB[0], hb_regs(0))
    norm_b2(u2Tb[0], R("u2T0"), 8)
    for n in range(NTB):
        hooks = {}
        if n > 0:
            hooks.update(ple_hooks(n - 1))
        if n + 1 < NTB:
            nb = (n + 1) % 2
            hooks[18] = lambda n=n: load_b(n + 1)
            hooks[24] = lambda nb=nb: norm_b1(HB[nb], hb_regs(nb))
            hooks[27] = lambda nb=nb: norm_b2(u2Tb[nb], R("u2T%d" % nb), 8)
        mlp(n, hooks)
    last = ple_hooks(NTB - 1)
    for k in sorted(last):
        last[k]()

    es = ExitStack()
    with es:
        P.emit(es, c_out)
    return nc


_CACHE = {}


def _get_program(S):
    if S not in _CACHE:
        _CACHE[S] = build_program(S)
    return _CACHE[S]


def kernel(x, p, norm_mix_g, w_in, conv_w, conv_b, w_rg, b_rg, w_ig, b_ig, lru_lambda, q_norm_g, k_norm_g,
           rel_bias, out_norm_lru_g, out_norm_attn_g, w_out, norm_mlp_g, w_up, w_down, norm_ple_g,
           w_ple_gate, w_ple_proj):
    f = np.float32
    x = np.asarray(x, f)
    p = np.asarray(p, f)
    Bsz, S, _ = x.shape

    def cols(v, n):
        return np.asarray(v, f).reshape(n, 128).T

    par = np.zeros((128, NPAR), f)
    par[:, 0:8] = cols(norm_mix_g[0], 8)
    par[:, 8:16] = cols(norm_mlp_g[0], 8)
    par[:, 16:24] = cols(norm_ple_g[0], 8)
    cw = np.asarray(conv_w[0], f)
    for c in range(4):
        for k in range(4):
            par[:, 24 + c * 4 + k] = cw[k, c * 128:(c + 1) * 128]
    par[:, 40:44] = cols(conv_b[0], 4)
    par[:, 44:48] = cols(b_rg[0], 4)
    par[:, 48:52] = cols(b_ig[0], 4)
    par[:, 52:56] = cols(lru_lambda[0], 4)
    par[:, 56:60] = cols(out_norm_lru_g[0], 4)
    par[:, 60:64] = cols(out_norm_attn_g[0], 4)
    par[:, 64] = np.tile(np.asarray(q_norm_g[0], f), 2)
    par[:, 65] = np.tile(np.asarray(k_norm_g[0], f), 2)

    def blockdiag(w):
        w = np.asarray(w, f)
        o = np.zeros((4, 128, 128), f)
        for n in range(8):
            c, r = n // 2, (n % 2) * 64
            o[c, r:r + 64, r:r + 64] = w[n]
        return o

    wrg = blockdiag(w_rg[0])
    wig = blockdiag(w_ig[0])
    pp_ = np.arange(128)[:, None]
    cc_ = np.arange(640)[None, :]
    idx = np.clip(cc_ - pp_, -256, 256) + 256
    btab = np.ascontiguousarray(np.asarray(rel_bias[0], f)[:, idx].transpose(1, 0, 2))
    dchunk = cc_ // 64 - pp_ // 64
    mask = ((dchunk >= 0) & (dchunk <= 8)).astype(f)
    ident = np.eye(128, dtype=f)
    bones = np.zeros((128, 128), f)
    bones[0:64, 0:64] = 1.0
    bones[64:128, 64:128] = 1.0

    shared = {
        "w_in": np.ascontiguousarray(w_in[0], f), "w_out": np.ascontiguousarray(w_out[0], f),
        "w_up": np.ascontiguousarray(w_up[0], f), "w_down": np.ascontiguousarray(w_down[0], f),
        "w_gate": np.ascontiguousarray(w_ple_gate[0], f), "w_ple": np.ascontiguousarray(w_ple_proj[0], f),
        "wrg": wrg, "wig": wig, "par": par, "btab": btab, "mask": mask, "ident": ident, "bones": bones,
    }
    nc = _get_program(S)
    in_maps = []
    for b in range(Bsz):
        m = dict(shared)
        m["x"] = np.ascontiguousarray(x[b])
        m["p"] = np.ascontiguousarray(p[0, b])
        in_maps.append(m)
    res = run_bass_kernel_spmd(nc, in_maps, core_ids=list(range(Bsz)))
    return np.stack([np.asarray(r["out"], f) for r in res.results], axis=0)
```

```python
import numpy as np
import concourse.bass as bass
import concourse.mybir as mybir
from concourse.bass_utils import run_bass_kernel_spmd

F32 = mybir.dt.float32
BF16 = mybir.dt.bfloat16
AF = mybir.ActivationFunctionType
ALU = mybir.AluOpType

D = 1024
NPAR = 66
EPS = 1e-6
ENGS = ("pe", "act", "dve", "pool", "sp")


class Reg:
    __slots__ = ("name", "excl", "writer", "readers", "last")

    def __init__(self, name, excl=False):
        self.name = name
        self.excl = excl
        self.writer = None
        self.readers = []
        self.last = None


class Chan:
    def __init__(self, sem):
        self.sem = sem
        self.count = 0
        self.last = None


class Ins:
    __slots__ = ("eng", "fn", "deps", "sig", "signals", "chan", "dval", "is_dma")

    def __init__(self, eng, fn):
        self.eng = eng
        self.fn = fn
        self.deps = []
        self.sig = 0
        self.signals = False
        self.chan = None
        self.dval = 0
        self.is_dma = False


class Prog:
    def __init__(self, nc):
        self.nc = nc
        self.q = {e: [] for e in ENGS}
        self.sems = {}
        self.chans = []
        self.regs = {}

    def R(self, name, excl=False):
        r = self.regs.get(name)
        if r is None:
            r = Reg(name, excl)
            self.regs[name] = r
        return r

    def chan(self):
        c = Chan(None)
        self.chans.append(c)
        return c

    def add(self, eng, fn, reads=(), writes=(), chan=None):
        ins = Ins(eng, fn)
        deps = {}

        def dep(d, raw):
            if d is None or d is ins:
                return
            if d.is_dma:
                deps[id(d)] = d
                return
            if d.eng == eng and (eng == "pe" or not raw):
                return
            deps[id(d)] = d

        for r in reads:
            dep(r.writer, True)
            if r.excl:
                dep(r.last, False)
        for w in writes:
            dep(w.writer, True)
            for rd in w.readers:
                dep(rd, False)
            if w.excl:
                dep(w.last, False)
        for r in reads:
            rl = r.readers
            if chan is None and rl and (not rl[-1].is_dma) and rl[-1].eng == eng:
                rl[-1] = ins
            else:
                for k in range(len(rl)):
                    if (not rl[k].is_dma) and rl[k].eng == eng and chan is None:
                        del rl[k]
                        break
                rl.append(ins)
            r.last = ins
        for w in writes:
            w.writer = ins
            w.readers = []
            w.last = ins
        ins.deps = list(deps.values())
        for d in ins.deps:
            d.signals = True
        if chan is not None:
            ins.is_dma = True
            ins.chan = chan
            chan.count += 16
            ins.dval = chan.count
            chan.last = ins
        self.q[eng].append(ins)
        return ins

    def barrier(self, chans):
        lasts = []
        for e in ENGS:
            for ins in reversed(self.q[e]):
                if not ins.is_dma and ins.fn is not None:
                    lasts.append(ins)
                    break
        dl = [c.last for c in chans if c.last is not None]
        for e in ENGS:
            ins = Ins(e, None)
            ins.deps = [d for d in lasts if d.eng != e] + dl
            for d in ins.deps:
                d.signals = True
            self.q[e].append(ins)

    def emit(self, es, final_chans):
        nc = self.nc
        for e in ENGS:
            self.sems[e] = es.enter_context(nc.semaphore("s_" + e))
        for k, c in enumerate(self.chans):
            c.sem = es.enter_context(nc.semaphore("c%d" % k))
        for e in ENGS:
            n = 0
            for ins in self.q[e]:
                if ins.signals and not ins.is_dma:
                    n += 1
                    ins.sig = n
        engobj = {"pe": "tensor", "act": "scalar", "dve": "vector", "pool": "gpsimd", "sp": "sync"}
        with nc.Block() as block:
            for e in ENGS:
                def body(eng, e=e):
                    seen = {}
                    for ins in self.q[e]:
                        for d in ins.deps:
                            if d.is_dma:
                                sem, val = d.chan.sem, d.dval
                            else:
                                sem, val = self.sems[d.eng], d.sig
                            k = id(sem)
                            if seen.get(k, 0) >= val:
                                continue
                            seen[k] = val
                            eng.wait_ge(sem, val)
                        if ins.fn is None:
                            continue
                        bi = ins.fn(eng)
                        if ins.is_dma:
                            bi.then_inc(ins.chan.sem, 16)
                        elif ins.signals:
                            bi.then_inc(self.sems[e], 1)
                    if e == "sp":
                        for ch in final_chans:
                            eng.wait_ge(ch.sem, ch.count)
                getattr(block, engobj[e])(body)


def MM(out, lhsT, rhs, start, stop):
    return lambda e: e.matmul(out, lhsT, rhs, start=start, stop=stop)


def TR(out, in_, ident):
    return lambda e: e.transpose(out, in_, ident)


def ACT(out, in_, func, bias=None, scale=1.0, accum=None):
    def f(e):
        kw = {}
        if bias is not None:
            kw["bias"] = bias
        if accum is not None:
            kw["accum_out"] = accum
        return e.activation(out=out, in_=in_, func=func, scale=scale, **kw)
    return f


def TT(out, a, b, op):
    return lambda e: e.tensor_tensor(out, a, b, op)


def TS(out, a, s1, s2, op0, op1):
    return lambda e: e.tensor_scalar(out, a, s1, s2, op0, op1)


def TSM(out, a, s):
    return lambda e: e.tensor_scalar_mul(out, a, s)


def STT(out, a, s, b, op0, op1):
    return lambda e: e.scalar_tensor_tensor(out, a, s, b, op0, op1)


def CP(out, a):
    return lambda e: e.tensor_copy(out, a)


def MEMSET(ap, v):
    return lambda e: e.memset(ap, v)


def RECIP(out, a):
    return lambda e: e.reciprocal(out, a)


def SCAN(out, d0, d1, init, op0, op1):
    return lambda e: e.tensor_tensor_scan(out, d0, d1, init, op0, op1)


def DMA(out, in_):
    return lambda e: e.dma_start(out=out, in_=in_)


class Arena:
    def __init__(self, nc, base, limit):
        self.nc = nc
        self.off = base
        self.limit = limit
        self.n = 0

    def alloc(self, name, shape, dt):
        esz = 2 if dt == BF16 else 4
        size = esz
        for s in shape[1:]:
            size *= s
        size = (size + 63) // 64 * 64
        assert self.off + size <= self.limit, ("SBUF overflow", name, self.off, size, self.limit)
        self.n += 1
        t = self.nc.alloc_sbuf_tensor_at("%s_%d" % (name, self.n), list(shape), dt, offset=self.off)
        self.off += size
        return t


def build_program(S):
    from contextlib import ExitStack
    NT = S // 512
    NTB = S // 256
    nc = bass.Bass("TRN2", target_bir_lowering=False)

    def din(name, shape):
        return nc.dram_tensor(name, list(shape), F32, kind="ExternalInput").ap()

    x_d = din("x", [S, D])
    p_d = din("p", [S, 256])
    win_d = din("w_in", [D, 2560])
    wout_d = din("w_out", [D, D])
    wup_d = din("w_up", [D, 4096])
    wdn_d = din("w_down", [4096, D])
    wg_d = din("w_gate", [D, D])
    wp_d = din("w_ple", [256, D])
    wrg_d = din("wrg", [4, 128, 128])
    wig_d = din("wig", [4, 128, 128])
    par_d = din("par", [128, NPAR])
    btab_d = din("btab", [128, 8, 640])
    mask_d = din("mask", [128, 640])
    ident_d = din("ident", [128, 128])
    bones_d = din("bones", [128, 128])
    out_d = nc.dram_tensor("out", [S, D], F32, kind="ExternalOutput").ap()
    h1_d = nc.dram_tensor("h1s", [S, D], F32, kind="Internal").ap()

    P = Prog(nc)
    R = P.R
    base = (nc._sbuf_addr_for_side("left") + 63) // 64 * 64
    limit = nc._sbuf_addr_for_side("right")
    A0 = Arena(nc, base, limit)

    par = A0.alloc("par", [128, NPAR], F32)
    ident = A0.alloc("ident", [128, 128], BF16)
    bones = A0.alloc("bones", [128, 128], BF16)
    onesc = A0.alloc("onesc", [128, 2], BF16)
    sca = A0.alloc("sca", [128, 4], F32)
    gg = A0.alloc("gg", [128, 1], F32)
    carry = A0.alloc("carry", [128, 4], F32)
    ss1 = A0.alloc("ss1", [128, 4], F32)
    rs1 = A0.alloc("rs1", [128, 4], F32)
    ssA = A0.alloc("ssA", [128, 4], F32)
    rsA = A0.alloc("rsA", [128, 4], F32)
    rsL = A0.alloc("rsL", [128, 4], F32)
    rden = A0.alloc("rden", [128, 4], F32)
    junk = A0.alloc("junk", [128, 1024], BF16)
    XS = [A0.alloc("xs", [128, 1024], BF16) for _ in range(2)]
    common_end = A0.off

    banks = [nc.alloc_psum_tensor("bank%d" % b, [128, 512], F32) for b in range(8)]
    PB = [R("bank%d" % b, True) for b in range(8)]

    def bf(b):
        return banks[b][:]

    def bb(b):
        return banks[b][:].bitcast(BF16)

    rot_state = [0]

    def rot():
        b = rot_state[0]
        rot_state[0] = (b + 1) % 4
        return b

    A = Arena(nc, common_end, limit)
    Win = A.alloc("Win", [128, 8, 2560], BF16)
    Wout = A.alloc("Wout", [128, 8, 1024], BF16)
    Wrg = A.alloc("Wrg", [128, 4, 128], BF16)
    Wig = A.alloc("Wig", [128, 4, 128], BF16)
    XH = [A.alloc("xh", [128, 4, 1024], F32) for _ in range(2)]
    uT = A.alloc("uT", [128, 8, 512], BF16)
    xle = A.alloc("xle", [128, 4, 516], F32)
    xc = [A.alloc("xc", [128, 512], F32) for _ in range(2)]
    xcb = [A.alloc("xcb", [128, 512], BF16) for _ in range(2)]
    rr0 = A.alloc("rr", [128, 512], F32)
    rr = [rr0, rr0]
    ig = [A.alloc("ig", [128, 512], F32) for _ in range(2)]
    aa = [A.alloc("aa", [128, 512], F32) for _ in range(2)]
    t1 = [A.alloc("t1", [128, 512], F32) for _ in range(2)]
    hh = [A.alloc("hh", [128, 512], F32) for _ in range(2)]
    gl = [A.alloc("gl", [128, 512], F32) for _ in range(2)]
    sqy = A.alloc("sqy", [128, 4, 512], BF16)
    mst_off = A.off
    qraw = [A.alloc("qraw", [128, 512], F32) for _ in range(2)]
    sqq = [A.alloc("sqq", [128, 512], BF16) for _ in range(2)]
    rsq = [A.alloc("rsq", [128, 512], F32) for _ in range(2)]
    qT = A.alloc("qT", [128, 4, 512], BF16)
    kT = A.alloc("kT", [128, 4, 1024], BF16)
    Vr = A.alloc("Vr", [128, 8, 8, 65], BF16)
    E = A.alloc("E", [128, 8, 640], BF16)
    stage_off = A.off
    Pb = [A.alloc("Pb", [128, 2560], BF16) for _ in range(2)]
    ygT = A.alloc("ygT", [128, 8, 512], BF16)
    yat = A.alloc("yat", [128, 4, 512], BF16)
    bst = nc.alloc_sbuf_tensor_at("bstage", [128, 8, 640], F32, offset=stage_off)
    mst = nc.alloc_sbuf_tensor_at("mstage", [128, 640], F32, offset=mst_off)
    assert stage_off + 8 * 640 * 4 <= A.off

    B = Arena(nc, common_end, limit)
    Wup = B.alloc("Wup", [128, 8, 4096], BF16)
    Wdn = B.alloc("Wdn", [128, 32, 1024], BF16)
    Wg = B.alloc("Wg", [128, 8, 1024], BF16)
    Wp = B.alloc("Wp", [128, 2, 1024], BF16)
    HB = [B.alloc("hb", [128, 2, 1024], F32) for _ in range(2)]
    PBF = [B.alloc("pb", [128, 2, 256], F32) for _ in range(2)]
    u2T = B.alloc("u2T", [128, 8, 256], BF16)
    rl = [B.alloc("rl", [128, 256], BF16) for _ in range(4)]
    hid = [B.alloc("hid", [128, 256], BF16) for _ in range(4)]
    gt = [B.alloc("gt", [128, 512], F32) for _ in range(2)]
    tmp = [B.alloc("tmp", [128, 512], F32) for _ in range(2)]
    pbb = [B.alloc("pbb", [128, 256], BF16) for _ in range(2)]
    pT = B.alloc("pT", [128, 2, 256], BF16)
    ssB = B.alloc("ssB", [128, 2], F32)
    rsB = B.alloc("rsB", [128, 2], F32)

    c_par = P.chan()
    c_small = P.chan()
    c_stage = P.chan()
    c_stage2 = P.chan()
    P.add("sp", DMA(par[:], par_d[:]), writes=[R("par")], chan=c_par)
    P.add("sp", DMA(bst[:], btab_d[:]), writes=[R("bst")], chan=c_stage)
    P.add("sp", DMA(mst[:], mask_d[:]), writes=[R("mst")], chan=c_stage2)
    P.add("pool", DMA(ident[:], ident_d[:]), writes=[R("ident")], chan=c_small)
    P.add("pool", DMA(bones[:], bones_d[:]), writes=[R("bones")], chan=c_small)
    P.add("pool", DMA(Wrg[:], wrg_d.rearrange("c p n -> p c n")), writes=[R("wrg")], chan=c_small)
    P.add("pool", DMA(Wig[:], wig_d.rearrange("c p n -> p c n")), writes=[R("wig")], chan=c_small)
    c_x = [P.chan(), P.chan()]

    def xh_regs(buf):
        return [R("xh%d_%d" % (buf, tg)) for tg in range(4)]

    def load_x(i):
        buf = i % 2
        src = x_d[i * 512:(i + 1) * 512, :].rearrange("(g p) f -> p g f", p=128)
        P.add("sp", DMA(XH[buf][:], src), writes=xh_regs(buf), chan=c_x[buf])

    load_x(0)
    win_v = win_d.rearrange("(k p) n -> p k n", p=128)
    c_win = []
    for pc in range(5):
        c = P.chan()
        c_win.append(c)
        P.add("pool", DMA(Win[:, :, pc * 512:(pc + 1) * 512], win_v[:, :, pc * 512:(pc + 1) * 512]),
              writes=[R("win%d" % pc)], chan=c)
    c_wout = P.chan()
    P.add("pool", DMA(Wout[:], wout_d.rearrange("(k p) n -> p k n", p=128)), writes=[R("wout")], chan=c_wout)

    P.add("dve", MEMSET(carry[:], 0.0), writes=[R("carry")])
    P.add("dve", MEMSET(xle[:], 0.0), writes=[R("xle%d" % c) for c in range(4)])
    P.add("dve", MEMSET(onesc[:], 1.0), writes=[R("onesc")])
    P.add("dve", MEMSET(Vr[:], 1.0), writes=[R("V%d" % s) for s in range(8)])
    P.add("act", ACT(sca[:], par[:, 52:56], AF.Exp, scale=-1.0), reads=[R("par")], writes=[R("sca")])
    P.add("act", ACT(sca[:], sca[:], AF.Ln, bias=1.0), reads=[R("sca")], writes=[R("sca")])
    P.add("dve", TSM(sca[:], sca[:], -8.0), reads=[R("sca")], writes=[R("sca")])
    P.add("dve", STT(gg[:], par[:, 64:65], 0.125, par[:, 65:66], ALU.mult, ALU.mult),
          reads=[R("par")], writes=[R("gg")])
    for h in range(8):
        P.add("act", ACT(bst[:, h, :], bst[:, h, :], AF.Exp), reads=[R("bst")], writes=[R("bst")])
        P.add("dve", TT(E[:, h, :], bst[:, h, :], mst[:], ALU.mult), reads=[R("bst"), R("mst")],
              writes=[R("E")])
    P.barrier([c_par, c_small, c_stage, c_stage2])

    c_h1s = [P.chan(), P.chan()]
    widths = []
    for b in range(8):
        ql = max(0, 2 * b - 8)
        qh = min(7, 2 * b + 1)
        widths.append((ql, (qh - ql + 1) * 64))
    poff = [0]
    for b in range(8):
        poff.append(poff[-1] + widths[b][1])

    def inproj_ws(col0, bank):
        pc = col0 // 512
        for k in range(8):
            P.add("pe", MM(bf(bank), Win[:, k, col0:col0 + 128], uT[:, k, :], k == 0, k == 7),
                  reads=[R("uT%d" % k), R("win%d" % pc)], writes=[PB[bank]])

    GC = 1.5957691216057308

    def norm1(i):
        cur = i % 2
        xh = XH[cur]
        xr = xh_regs(cur)
        P.add("dve", MEMSET(ss1[:], 0.0), writes=[R("ss1")])
        for tg in range(4):
            P.add("act", ACT(junk[:], xh[:, tg, :], AF.Square, accum=ss1[:, tg:tg + 1]),
                  reads=[xr[tg]], writes=[R("junk"), R("ss1")])
        P.add("act", ACT(rs1[:], ss1[:], AF.Ln, bias=EPS, scale=1.0 / D), reads=[R("ss1")], writes=[R("rs1")])
        P.add("act", ACT(rs1[:], rs1[:], AF.Exp, scale=-0.5), reads=[R("rs1")], writes=[R("rs1")])
        for tg in range(4):
            xs = XS[tg % 2]
            xsr = R("xs%d" % (tg % 2))
            P.add("dve", TSM(xs[:], xh[:, tg, :], rs1[:, tg:tg + 1]), reads=[xr[tg], R("rs1")], writes=[xsr])
            for c in range(8):
                bank = 4 + c // 2
                o = (c % 2) * 512 + tg * 128
                P.add("pe", TR(bb(bank)[:, o:o + 128], xs[:, c * 128:(c + 1) * 128], ident[:]),
                      reads=[xsr, R("ident")], writes=[PB[bank]])
        for c in range(8):
            bank = 4 + c // 2
            o = (c % 2) * 512
            P.add("dve", TSM(uT[:, c, :], bb(bank)[:, o:o + 512], par[:, c:c + 1]),
                  reads=[PB[bank], R("par")], writes=[R("uT%d" % c)])

    def mixer(i):
        cur = i % 2
        prv = 1 - cur

        def lru_F(c):
            s = c % 2
            b1 = rot()
            inproj_ws(c * 128, b1)
            P.add("act", ACT(xle[:, c, 3:515], bf(b1), AF.Copy), reads=[PB[b1]], writes=[R("xle%d" % c)])
            b2 = rot()
            inproj_ws(512 + c * 128, b2)
            P.add("act", ACT(gl[s][:], bf(b2), AF.Copy), reads=[PB[b2]], writes=[R("gl%d" % s)])
            rx = [R("xle%d" % c), R("par")]
            P.add("pool", TS(xc[s][:], xle[:, c, 0:512], par[:, 24 + c * 4:25 + c * 4], par[:, 40 + c:41 + c],
                             ALU.mult, ALU.add), reads=rx, writes=[R("xc%d" % s)])
            for k in range(1, 4):
                P.add("dve", STT(xc[s][:], xle[:, c, k:k + 512], par[:, 24 + c * 4 + k:25 + c * 4 + k], xc[s][:],
                                 ALU.mult, ALU.add), reads=rx + [R("xc%d" % s)], writes=[R("xc%d" % s)])
            P.add("pool", CP(xle[:, c, 0:3], xle[:, c, 512:515]), reads=[R("xle%d" % c)], writes=[R("xle%d" % c)])
            P.add("dve", CP(xcb[s][:], xc[s][:]), reads=[R("xc%d" % s)], writes=[R("xcb%d" % s)])
            P.add("pool", TT(t1[s][:], gl[s][:], gl[s][:], ALU.mult), reads=[R("gl%d" % s)], writes=[R("t1%d" % s)])
            P.add("pool", TS(t1[s][:], t1[s][:], 0.044715, 1.0, ALU.mult, ALU.add), reads=[R("t1%d" % s)],
                  writes=[R("t1%d" % s)])
            P.add("pool", TT(t1[s][:], t1[s][:], gl[s][:], ALU.mult), reads=[R("t1%d" % s), R("gl%d" % s)],
                  writes=[R("t1%d" % s)])

        def lru_S(c):
            s = c % 2
            br = rot()
            P.add("pe", MM(bf(br), Wrg[:, c, :], xcb[s][:], True, True), reads=[R("xcb%d" % s), R("wrg")],
                  writes=[PB[br]])
            P.add("act", ACT(aa[s][:], bf(br), AF.Sigmoid, bias=par[:, 44 + c:45 + c]), reads=[PB[br], R("par")],
                  writes=[R("aa%d" % s)])
            bi = rot()
            P.add("pe", MM(bf(bi), Wig[:, c, :], xcb[s][:], True, True), reads=[R("xcb%d" % s), R("wig")],
                  writes=[PB[bi]])
            P.add("act", ACT(ig[s][:], bf(bi), AF.Sigmoid, bias=par[:, 48 + c:49 + c]), reads=[PB[bi], R("par")],
                  writes=[R("ig%d" % s)])
            P.add("act", ACT(t1[s][:], t1[s][:], AF.Sigmoid, scale=GC), reads=[R("t1%d" % s)], writes=[R("t1%d" % s)])
            P.add("dve", TT(gl[s][:], gl[s][:], t1[s][:], ALU.mult), reads=[R("gl%d" % s), R("t1%d" % s)],
                  writes=[R("gl%d" % s)])

        def lru_E(c):
            s = c % 2
            P.add("act", ACT(aa[s][:], aa[s][:], AF.Exp, scale=sca[:, c:c + 1]), reads=[R("aa%d" % s), R("sca")],
                  writes=[R("aa%d" % s)])
            P.add("dve", STT(t1[s][:], aa[s][:], -1.0, aa[s][:], ALU.mult, ALU.mult), reads=[R("aa%d" % s)],
                  writes=[R("t1%d" % s)])
            P.add("dve", TS(t1[s][:], t1[s][:], 1.0, 1e-12, ALU.add, ALU.max), reads=[R("t1%d" % s)],
                  writes=[R("t1%d" % s)])
            P.add("act", ACT(t1[s][:], t1[s][:], AF.Ln), reads=[R("t1%d" % s)], writes=[R("t1%d" % s)])
            P.add("act", ACT(t1[s][:], t1[s][:], AF.Exp, scale=0.5), reads=[R("t1%d" % s)], writes=[R("t1%d" % s)])
            P.add("dve", TT(ig[s][:], ig[s][:], xc[s][:], ALU.mult), reads=[R("ig%d" % s), R("xc%d" % s)],
                  writes=[R("ig%d" % s)])
            P.add("dve", TT(ig[s][:], ig[s][:], t1[s][:], ALU.mult), reads=[R("ig%d" % s), R("t1%d" % s)],
                  writes=[R("ig%d" % s)])
            P.add("dve", SCAN(hh[s][:], aa[s][:], ig[s][:], carry[:, c:c + 1], ALU.mult, ALU.add),
                  reads=[R("aa%d" % s), R("ig%d" % s), R("carry")], writes=[R("hh%d" % s)])
            P.add("dve", CP(carry[:, c:c + 1], hh[s][:, 511:512]), reads=[R("hh%d" % s)], writes=[R("carry")])
            P.add("dve", TT(gl[s][:], gl[s][:], hh[s][:], ALU.mult), reads=[R("gl%d" % s), R("hh%d" % s)],
                  writes=[R("gl%d" % s)])
            P.add("act", ACT(sqy[:, c, :], gl[s][:], AF.Square), reads=[R("gl%d" % s)], writes=[R("sqy%d" % c)])
            P.add("dve", TSM(ygT[:, c, :], gl[s][:], par[:, 56 + c:57 + c]), reads=[R("gl%d" % s), R("par")],
                  writes=[R("ygT%d" % c)])

        def qk_front(n):
            s = n % 2
            col0 = 1024 + n * 128
            b = rot()
            inproj_ws(col0, b)
            P.add("dve", CP(qraw[s][:], bf(b)), reads=[PB[b]], writes=[R("qraw%d" % s)])
            P.add("act", ACT(sqq[s][:], qraw[s][:], AF.Square), reads=[R("qraw%d" % s)], writes=[R("sqq%d" % s)])

        def qk_back(n):
            s = n % 2
            b2 = rot()
            P.add("pe", MM(bf(b2), bones[:], sqq[s][:], True, True), reads=[R("sqq%d" % s), R("bones")],
                  writes=[PB[b2]])
            P.add("act", ACT(rsq[s][:], bf(b2), AF.Ln, bias=EPS, scale=1.0 / 64), reads=[PB[b2]],
                  writes=[R("rsq%d" % s)])
            P.add("act", ACT(rsq[s][:], rsq[s][:], AF.Exp, scale=-0.5), reads=[R("rsq%d" % s)],
                  writes=[R("rsq%d" % s)])
            if n < 4:
                P.add("dve", STT(qT[:, n, :], qraw[s][:], gg[:, 0:1], rsq[s][:], ALU.mult, ALU.mult),
                      reads=[R("qraw%d" % s), R("gg"), R("rsq%d" % s)], writes=[R("qT%d" % n)])
            else:
                j = n - 4
                P.add("dve", TT(kT[:, j, cur * 512:(cur + 1) * 512], qraw[s][:], rsq[s][:], ALU.mult),
                      reads=[R("qraw%d" % s), R("rsq%d" % s)], writes=[R("kT%d_%d" % (j, cur))])

        qk_front(0)
        for n in range(1, 8):
            qk_front(n)
            qk_back(n - 1)
        for tg in range(4):
            b = rot()
            for k in range(8):
                P.add("pe", MM(bf(b), uT[:, k, tg * 128:(tg + 1) * 128], Win[:, k, 2048:2560], k == 0, k == 7),
                      reads=[R("uT%d" % k), R("win4")], writes=[PB[b]])
            slot = 4 * cur + tg
            P.add("act", ACT(Vr[:, slot, :, 0:64], bf(b).rearrange("p (h d) -> p h d", h=8), AF.Copy),
                  reads=[PB[b]], writes=[R("V%d" % slot)])
            if tg == 0:
                qk_back(7)

        blocks = [b for b in range(8) if (i > 0 or b >= 4)]

        def attn_head(h):
            j = h // 2
            ro = 64 * (h % 2)
            st = h % 2
            hq = h % 4
            for b in blocks:
                ql, w = widths[b]
                half = prv if b < 4 else cur
                kcol = half * 512 + (b % 4) * 128
                bank = rot()
                P.add("pe", MM(bf(bank)[:, 0:w], kT[ro:ro + 64, j, kcol:kcol + 128],
                               qT[ro:ro + 64, j, ql * 64:ql * 64 + w], True, True),
                      reads=[R("kT%d_%d" % (j, half)), R("qT%d" % j)], writes=[PB[bank]])
                pr = R("P%d_%d" % (st, b))
                pv = Pb[st][:, poff[b]:poff[b] + w]
                P.add("act", ACT(pv, bf(bank)[:, 0:w], AF.Exp), reads=[PB[bank]], writes=[pr])
                ec = (ql + 8 - 2 * b) * 64
                P.add("dve", TT(pv, pv, E[:, h, ec:ec + w], ALU.mult), reads=[pr, R("E")], writes=[pr])

        def attn_pv(h):
            st = h % 2
            hq = h % 4
            for g in range(4):
                bl = [b for b in range(g, g + 5) if b in blocks]
                for n, b in enumerate(bl):
                    ql, w = widths[b]
                    half = prv if b < 4 else cur
                    slot = 4 * half + (b % 4)
                    pcn = poff[b] + (2 * g - ql) * 64
                    P.add("pe", MM(bf(4 + g)[:, hq * 65:(hq + 1) * 65], Pb[st][:, pcn:pcn + 128], Vr[:, slot, h, :],
                                   n == 0, n == len(bl) - 1),
                          reads=[R("P%d_%d" % (st, b)), R("V%d" % slot)], writes=[PB[4 + g]])
            if hq == 3:
                hb0 = (h // 4) * 4
                for g in range(4):
                    v3 = bf(4 + g)[:, 0:260].rearrange("p (h e) -> p h e", e=65)
                    P.add("dve", RECIP(rden[:], v3[:, :, 64]), reads=[PB[4 + g]], writes=[R("rden")])
                    P.add("dve", TT(yat[:, g, hb0 * 64:(hb0 + 4) * 64].rearrange("p (h d) -> p h d", d=64),
                                    v3[:, :, 0:64], rden[:].unsqueeze(2).to_broadcast([128, 4, 64]), ALU.mult),
                          reads=[PB[4 + g], R("rden")], writes=[R("yat%d" % g)])
        sched = [(0, (lru_F, 0), (lru_F, 1)), (1, (lru_S, 0), (lru_S, 1)), (2, (lru_E, 0), (lru_E, 1)),
                 (3, (lru_F, 2), (lru_F, 3)), (4, (lru_S, 2), (lru_S, 3)), (5, (lru_E, 2), (lru_E, 3)),
                 (6,), (7,)]
        attn_head(0)
        for item in sched:
            if item[0] + 1 < 8:
                attn_head(item[0] + 1)
            attn_pv(item[0])
            for fn, c in item[1:]:
                fn(c)
        bs = rot()
        for tg in range(4):
            for c in range(4):
                P.add("pe", MM(bf(bs)[:, tg:tg + 1], sqy[:, c, tg * 128:(tg + 1) * 128], onesc[:, 0:1], c == 0, c == 3),
                      reads=[R("sqy%d" % c), R("onesc")], writes=[PB[bs]])
        P.add("act", ACT(rsL[:], bf(bs)[:, 0:4], AF.Ln, bias=EPS, scale=1.0 / 512), reads=[PB[bs]], writes=[R("rsL")])
        P.add("act", ACT(rsL[:], rsL[:], AF.Exp, scale=-0.5), reads=[R("rsL")], writes=[R("rsL")])

        P.add("dve", MEMSET(ssA[:], 0.0), writes=[R("ssA")])
        for g in range(4):
            P.add("act", ACT(junk[:, 0:512], yat[:, g, :], AF.Square, accum=ssA[:, g:g + 1]),
                  reads=[R("yat%d" % g)], writes=[R("junk"), R("ssA")])
        P.add("act", ACT(rsA[:], ssA[:], AF.Ln, bias=EPS, scale=1.0 / 512), reads=[R("ssA")], writes=[R("rsA")])
        P.add("act", ACT(rsA[:], rsA[:], AF.Exp, scale=-0.5), reads=[R("rsA")], writes=[R("rsA")])
        for fc in range(4):
            bank = 4 + fc // 2
            for g in range(4):
                o = (fc % 2) * 512 + g * 128
                P.add("pe", TR(bb(bank)[:, o:o + 128], yat[:, g, fc * 128:(fc + 1) * 128], ident[:]),
                      reads=[R("yat%d" % g), R("ident")], writes=[PB[bank]])
        for fc in range(4):
            bank = 4 + fc // 2
            o = (fc % 2) * 512
            P.add("dve", TSM(ygT[:, 4 + fc, :], bb(bank)[:, o:o + 512], par[:, 60 + fc:61 + fc]),
                  reads=[PB[bank], R("par")], writes=[R("ygT%d" % (4 + fc))])

    def outproj(i):
        cur = i % 2
        xh = XH[cur]
        xr = xh_regs(cur)
        for tg in range(4):
            for half in range(2):
                hv = xh[:, tg, half * 512:(half + 1) * 512]
                bA = rot()
                for k in range(4):
                    P.add("pe", MM(bf(bA), ygT[:, k, tg * 128:(tg + 1) * 128], Wout[:, k, half * 512:(half + 1) * 512],
                                   k == 0, k == 3), reads=[R("ygT%d" % k), R("wout")], writes=[PB[bA]])
                bB = rot()
                for k in range(4, 8):
                    P.add("pe", MM(bf(bB), ygT[:, k, tg * 128:(tg + 1) * 128], Wout[:, k, half * 512:(half + 1) * 512],
                                   k == 4, k == 7), reads=[R("ygT%d" % k), R("wout")], writes=[PB[bB]])
                P.add("dve", STT(hv, bf(bA), rsL[:, tg:tg + 1], hv, ALU.mult, ALU.add),
                      reads=[PB[bA], R("rsL"), xr[tg]], writes=[xr[tg]])
                P.add("dve", STT(hv, bf(bB), rsA[:, tg:tg + 1], hv, ALU.mult, ALU.add),
                      reads=[PB[bB], R("rsA"), xr[tg]], writes=[xr[tg]])
        dst = h1_d[i * 512:(i + 1) * 512, :].rearrange("(g p) f -> p g f", p=128)
        P.add("sp", DMA(dst, xh[:]), reads=xr, writes=[R("h1_%d" % i)], chan=c_h1s[cur])

    norm1(0)
    for i in range(NT):
        if i + 1 < NT:
            load_x(i + 1)
        mixer(i)
        if i + 1 < NT:
            norm1(i + 1)
        outproj(i)

    P.barrier(c_h1s + c_x + c_win + [c_wout])

    c_hb = [P.chan(), P.chan()]
    c_pb = [P.chan(), P.chan()]
    c_out = [P.chan(), P.chan()]
    wup_v = wup_d.rearrange("(k p) n -> p k n", p=128)
    wdn_v = wdn_d.rearrange("(f p) n -> p f n", p=128)
    c_w = []

    def hb_regs(buf):
        return [R("hb%d_%d" % (buf, tg)) for tg in range(2)]

    def load_b(n):
        buf = n % 2
        src = h1_d[n * 256:(n + 1) * 256, :].rearrange("(g p) f -> p g f", p=128)
        P.add("sp", DMA(HB[buf][:], src), reads=[R("h1_%d" % (n // 2))], writes=hb_regs(buf), chan=c_hb[buf])
        srcp = p_d[n * 256:(n + 1) * 256, :].rearrange("(g p) f -> p g f", p=128)
        P.add("sp", DMA(PBF[buf][:], srcp), writes=[R("pb%d" % buf)], chan=c_pb[buf])

    load_b(0)
    for pc in range(8):
        c = P.chan()
        c_w.append(c)
        P.add("pool", DMA(Wup[:, :, pc * 512:(pc + 1) * 512], wup_v[:, :, pc * 512:(pc + 1) * 512]),
              writes=[R("wup%d" % pc)], chan=c)
        c = P.chan()
        c_w.append(c)
        P.add("pool", DMA(Wdn[:, pc * 4:(pc + 1) * 4, :], wdn_v[:, pc * 4:(pc + 1) * 4, :]),
              writes=[R("wdn%d" % pc)], chan=c)
    c = P.chan()
    c_w.append(c)
    P.add("pool", DMA(Wg[:], wg_d.rearrange("(k p) n -> p k n", p=128)), writes=[R("wg")], chan=c)
    c = P.chan()
    c_w.append(c)
    P.add("pool", DMA(Wp[:], wp_d.rearrange("(k p) n -> p k n", p=128)), writes=[R("wp")], chan=c)

    u2Tb = [u2T, B.alloc("u2Tb", [128, 8, 256], BF16)]
    u3T = B.alloc("u3T", [128, 8, 256], BF16)

    def norm_b1(hb, hr):
        P.add("dve", MEMSET(ssB[:], 0.0), writes=[R("ssB")])
        for tg in range(2):
            P.add("act", ACT(junk[:], hb[:, tg, :], AF.Square, accum=ssB[:, tg:tg + 1]),
                  reads=[hr[tg]], writes=[R("junk"), R("ssB")])
        P.add("act", ACT(rsB[:], ssB[:], AF.Ln, bias=EPS, scale=1.0 / D), reads=[R("ssB")], writes=[R("rsB")])
        P.add("act", ACT(rsB[:], rsB[:], AF.Exp, scale=-0.5), reads=[R("rsB")], writes=[R("rsB")])
        for tg in range(2):
            P.add("dve", TSM(XS[tg][:], hb[:, tg, :], rsB[:, tg:tg + 1]), reads=[hr[tg], R("rsB")],
                  writes=[R("xs%d" % tg)])

    def norm_b2(dstT, dreg, gcol):
        bT = [rot(), rot()]
        for tg in range(2):
            for c in range(8):
                bank = bT[c // 4]
                o = (c % 4) * 256 + tg * 128
                P.add("pe", TR(bb(bank)[:, o:o + 128], XS[tg][:, c * 128:(c + 1) * 128], ident[:]),
                      reads=[R("xs%d" % tg), R("ident")], writes=[PB[bank]])
        for c in range(8):
            bank = bT[c // 4]
            o = (c % 4) * 256
            P.add("dve", TSM(dstT[:, c, :], bb(bank)[:, o:o + 256], par[:, gcol + c:gcol + c + 1]),
                  reads=[PB[bank], R("par")], writes=[dreg])

    def ple_a(n):
        cur = n % 2
        pbf = PBF[cur]
        bt = rot()
        for tg in range(2):
            P.add("dve", CP(pbb[tg][:], pbf[:, tg, :]), reads=[R("pb%d" % cur)], writes=[R("pbb%d" % tg)])
            for kc in range(2):
                o = kc * 256 + tg * 128
                P.add("pe", TR(bb(bt)[:, o:o + 128], pbb[tg][:, kc * 128:(kc + 1) * 128], ident[:]),
                      reads=[R("pbb%d" % tg), R("ident")], writes=[PB[bt]])
        P.add("act", ACT(pT[:].rearrange("p k t -> p (k t)"), bb(bt)[:, 0:512], AF.Copy), reads=[PB[bt]],
              writes=[R("pT")])

    def ple_b(n, tg, half):
        cur = n % 2
        hb = HB[cur]
        hr = hb_regs(cur)
        s = (tg * 2 + half) % 2
        hv = hb[:, tg, half * 512:(half + 1) * 512]
        bgt = rot()
        for k in range(8):
            P.add("pe", MM(bf(bgt), u3T[:, k, tg * 128:(tg + 1) * 128], Wg[:, k, half * 512:(half + 1) * 512],
                           k == 0, k == 7), reads=[R("u3T"), R("wg")], writes=[PB[bgt]])
        P.add("act", ACT(gt[s][:], bf(bgt), AF.Sigmoid), reads=[PB[bgt]], writes=[R("gt%d" % s)])
        bpp = rot()
        for kc in range(2):
            P.add("pe", MM(bf(bpp), pT[:, kc, tg * 128:(tg + 1) * 128], Wp[:, kc, half * 512:(half + 1) * 512],
                           kc == 0, kc == 1), reads=[R("pT"), R("wp")], writes=[PB[bpp]])
        P.add("dve", TT(tmp[s][:], bf(bpp), gt[s][:], ALU.mult), reads=[R("gt%d" % s), PB[bpp]],
              writes=[R("tmp%d" % s)])
        P.add("dve", TT(hv, hv, tmp[s][:], ALU.add), reads=[hr[tg], R("tmp%d" % s)], writes=[hr[tg]])

    def store_b(n):
        cur = n % 2
        dst = out_d[n * 256:(n + 1) * 256, :].rearrange("(g p) f -> p g f", p=128)
        P.add("sp", DMA(dst, HB[cur][:]), reads=hb_regs(cur), chan=c_out[cur])

    def ple_hooks(n):
        hb = HB[n % 2]
        hr = hb_regs(n % 2)
        return {
            1: lambda: norm_b1(hb, hr),
            4: lambda: (norm_b2(u3T, R("u3T"), 16), ple_a(n)),
            8: lambda: ple_b(n, 0, 0),
            10: lambda: ple_b(n, 0, 1),
            12: lambda: ple_b(n, 1, 0),
            14: lambda: (ple_b(n, 1, 1), store_b(n)),
        }

    def mlp(n, hooks):
        cur = n % 2
        hb = HB[cur]
        hr = hb_regs(cur)
        uu = u2Tb[cur]
        ur = R("u2T%d" % cur)

        def up(f):
            b = rot()
            for k in range(8):
                P.add("pe", MM(bf(b)[:, 0:256], Wup[:, k, f * 128:(f + 1) * 128], uu[:, k, :], k == 0, k == 7),
                      reads=[ur, R("wup%d" % (f // 4))], writes=[PB[b]])
            P.add("act", ACT(rl[f % 4][:], bf(b)[:, 0:256], AF.Relu), reads=[PB[b]], writes=[R("rl%d" % (f % 4))])
            P.add("dve", TT(hid[f % 4][:], rl[f % 4][:], rl[f % 4][:], ALU.mult), reads=[R("rl%d" % (f % 4))],
                  writes=[R("hid%d" % (f % 4))])

        def down(f):
            for tg in range(2):
                for half in range(2):
                    bk = 4 + tg * 2 + half
                    P.add("pe", MM(bf(bk), hid[f % 4][:, tg * 128:(tg + 1) * 128],
                                   Wdn[:, f, half * 512:(half + 1) * 512], f == 0, f == 31),
                          reads=[R("hid%d" % (f % 4)), R("wdn%d" % (f // 4))], writes=[PB[bk]])

        up(0)
        up(1)
        for f in range(32):
            if f in hooks:
                hooks[f]()
            if f + 2 < 32:
                up(f + 2)
            down(f)
        for tg in range(2):
            for half in range(2):
                bk = 4 + tg * 2 + half
                hv = hb[:, tg, half * 512:(half + 1) * 512]
                P.add("dve", TT(hv, bf(bk), hv, ALU.add), reads=[hr[tg], PB[bk]], writes=[hr[tg]])

    norm_b1(HB[0], hb_regs(0))
    norm_b2(u2Tb[0], R("u2T0"), 8)
    for n in range(NTB):
        hooks = {}
        if n > 0:
            hooks.update(ple_hooks(n - 1))
        if n + 1 < NTB:
            nb = (n + 1) % 2
            hooks[18] = lambda n=n: load_b(n + 1)
            hooks[24] = lambda nb=nb: norm_b1(HB[nb], hb_regs(nb))
            hooks[27] = lambda nb=nb: norm_b2(u2Tb[nb], R("u2T%d" % nb), 8)
        mlp(n, hooks)
    last = ple_hooks(NTB - 1)
    for k in sorted(last):
        last[k]()

    es = ExitStack()
    with es:
        P.emit(es, c_out)
    return nc


_CACHE = {}


def _get_program(S):
    if S not in _CACHE:
        _CACHE[S] = build_program(S)
    return _CACHE[S]


def kernel(x, p, norm_mix_g, w_in, conv_w, conv_b, w_rg, b_rg, w_ig, b_ig, lru_lambda, q_norm_g, k_norm_g,
           rel_bias, out_norm_lru_g, out_norm_attn_g, w_out, norm_mlp_g, w_up, w_down, norm_ple_g,
           w_ple_gate, w_ple_proj):
    f = np.float32
    x = np.asarray(x, f)
    p = np.asarray(p, f)
    Bsz, S, _ = x.shape

    def cols(v, n):
        return np.asarray(v, f).reshape(n, 128).T

    par = np.zeros((128, NPAR), f)
    par[:, 0:8] = cols(norm_mix_g[0], 8)
    par[:, 8:16] = cols(norm_mlp_g[0], 8)
    par[:, 16:24] = cols(norm_ple_g[0], 8)
    cw = np.asarray(conv_w[0], f)
    for c in range(4):
        for k in range(4):
            par[:, 24 + c * 4 + k] = cw[k, c * 128:(c + 1) * 128]
    par[:, 40:44] = cols(conv_b[0], 4)
    par[:, 44:48] = cols(b_rg[0], 4)
    par[:, 48:52] = cols(b_ig[0], 4)
    par[:, 52:56] = cols(lru_lambda[0], 4)
    par[:, 56:60] = cols(out_norm_lru_g[0], 4)
    par[:, 60:64] = cols(out_norm_attn_g[0], 4)
    par[:, 64] = np.tile(np.asarray(q_norm_g[0], f), 2)
    par[:, 65] = np.tile(np.asarray(k_norm_g[0], f), 2)

    def blockdiag(w):
        w = np.asarray(w, f)
        o = np.zeros((4, 128, 128), f)
        for n in range(8):
            c, r = n // 2, (n % 2) * 64
            o[c, r:r + 64, r:r + 64] = w[n]
        return o

    wrg = blockdiag(w_rg[0])
    wig = blockdiag(w_ig[0])
    pp_ = np.arange(128)[:, None]
    cc_ = np.arange(640)[None, :]
    idx = np.clip(cc_ - pp_, -256, 256) + 256
    btab = np.ascontiguousarray(np.asarray(rel_bias[0], f)[:, idx].transpose(1, 0, 2))
    dchunk = cc_ // 64 - pp_ // 64
    mask = ((dchunk >= 0) & (dchunk <= 8)).astype(f)
    ident = np.eye(128, dtype=f)
    bones = np.zeros((128, 128), f)
    bones[0:64, 0:64] = 1.0
    bones[64:128, 64:128] = 1.0

    shared = {
        "w_in": np.ascontiguousarray(w_in[0], f), "w_out": np.ascontiguousarray(w_out[0], f),
        "w_up": np.ascontiguousarray(w_up[0], f), "w_down": np.ascontiguousarray(w_down[0], f),
        "w_gate": np.ascontiguousarray(w_ple_gate[0], f), "w_ple": np.ascontiguousarray(w_ple_proj[0], f),
        "wrg": wrg, "wig": wig, "par": par, "btab": btab, "mask": mask, "ident": ident, "bones": bones,
    }
    nc = _get_program(S)
    in_maps = []
    for b in range(Bsz):
        m = dict(shared)
        m["x"] = np.ascontiguousarray(x[b])
        m["p"] = np.ascontiguousarray(p[0, b])
        in_maps.append(m)
    res = run_bass_kernel_spmd(nc, in_maps, core_ids=list(range(Bsz)))
    return np.stack([np.asarray(r["out"], f) for r in res.results], axis=0)
```
